# Optimizing a Trainium2 kernel written in Bass

```python
import math
import jax, jax.numpy as jnp
from jax import lax
import numpy as np

D_MODEL = 1024
BATCH = 8
SEQ = 8192
DEPTH = 2
DEC_BATCH = 16
DEC_SEQ = 64
PAST_LEN = 2048

CHUNK = 64
D_CONV = D_MODEL
CONV_WIDTH = 31
N_HEADS = 8
HEAD_DIM = 64
V_DIM = 2 * HEAD_DIM
ROT_DIM = HEAD_DIM // 4
ROPE_THETA = 500000.0
D_FF = 4 * D_MODEL
Q_BLOCK = 128
LN_EPS = 1e-5
ALPHA = (2 * DEPTH) ** 0.25
BETA = (8 * DEPTH) ** -0.25
QK_W = N_HEADS * 2 * HEAD_DIM
ATTN_W = N_HEADS * V_DIM
OFF_Q = 2 * D_CONV
OFF_K = OFF_Q + QK_W
OFF_V = OFF_K + QK_W
OFF_G = OFF_V + ATTN_W
IN_COLS = OFF_G + 2 * D_MODEL

kernel_name = "hybrid_conformer_diffattn_streaming_step"


def layer_norm(x, g, b):
    xf = x.astype(jnp.float32)
    mu = jnp.mean(xf, axis=-1, keepdims=True)
    var = jnp.mean(jnp.square(xf - mu), axis=-1, keepdims=True)
    y = (xf - mu) * lax.rsqrt(var + LN_EPS)
    return (y * g.astype(jnp.float32) + b.astype(jnp.float32)).astype(x.dtype)


def rms_norm(x, g):
    xf = x.astype(jnp.float32)
    y = xf * lax.rsqrt(jnp.mean(xf * xf, axis=-1, keepdims=True) + LN_EPS)
    return (y * g.astype(jnp.float32)).astype(x.dtype)


def rope_partial(t, pos):
    half = ROT_DIM // 2
    inv = ROPE_THETA ** (-jnp.arange(half, dtype=jnp.float32) * 2.0 / ROT_DIM)
    ang = pos.astype(jnp.float32)[:, None] * inv[None, :]
    cos = jnp.cos(ang)[:, None, None, :]
    sin = jnp.sin(ang)[:, None, None, :]
    tf = t.astype(jnp.float32)
    t1 = tf[..., :half]
    t2 = tf[..., half:ROT_DIM]
    out = jnp.concatenate([t1 * cos - t2 * sin, t2 * cos + t1 * sin, tf[..., ROT_DIM:]], axis=-1)
    return out.astype(t.dtype)


def diff_attend(q, k, v, q_pos, k_pos, lam):
    s = jnp.einsum('bqhcd,bkhcd->bchqk', q, k, preferred_element_type=jnp.float32) * (HEAD_DIM ** -0.5)
    visible = (q_pos[:, None] // CHUNK) >= (k_pos[None, :] // CHUNK)
    s = jnp.where(visible, s, -jnp.inf)
    p = jax.nn.softmax(s, axis=-1)
    a = p[:, 0] - lam * p[:, 1]
    return jnp.einsum('bhqk,bkhe->bqhe', a.astype(v.dtype), v)


def attn_prompt(q, k, v, lam):
    b, s = q.shape[0], q.shape[1]
    nb = s // Q_BLOCK
    qb = q.reshape(b, nb, Q_BLOCK, N_HEADS, 2, HEAD_DIM).swapaxes(0, 1)
    pos = jnp.arange(s, dtype=jnp.int32)
    pb = pos.reshape(nb, Q_BLOCK)
    out = lax.map(lambda a: diff_attend(a[0], k, v, a[1], pos, lam), (qb, pb))
    return out.swapaxes(0, 1).reshape(b, s, N_HEADS, V_DIM)


def layer_step(x, conv_hist, k_past, v_past, prm, lam_init):
    b, s = x.shape[0], x.shape[1]
    u = x @ prm['w_in']

    glu = u[..., :D_CONV] * jax.nn.sigmoid(u[..., D_CONV:OFF_Q])
    xp = jnp.concatenate([conv_hist.astype(glu.dtype), glu], axis=1)
    dw = lax.conv_general_dilated(xp, prm['conv_dw_w'][:, None, :], window_strides=(1,), padding='VALID',
                                  dimension_numbers=('NWC', 'WIO', 'NWC'), feature_group_count=D_CONV)
    dw = dw + prm['conv_dw_b']
    h = jax.nn.silu(layer_norm(dw, prm['conv_ln_g'], prm['conv_ln_b']))
    conv_out = h @ prm['w_conv_out']
    new_conv = xp[:, -(CONV_WIDTH - 1):]

    p0 = 0 if k_past is None else k_past.shape[1]
    pos = p0 + jnp.arange(s, dtype=jnp.int32)
    q = rope_partial(u[..., OFF_Q:OFF_K].reshape(b, s, N_HEADS, 2, HEAD_DIM), pos)
    k = rope_partial(u[..., OFF_K:OFF_V].reshape(b, s, N_HEADS, 2, HEAD_DIM), pos)
    v = u[..., OFF_V:OFF_G].reshape(b, s, N_HEADS, V_DIM)
    lq1 = prm['lambda_q1'].astype(jnp.float32)
    lk1 = prm['lambda_k1'].astype(jnp.float32)
    lq2 = prm['lambda_q2'].astype(jnp.float32)
    lk2 = prm['lambda_k2'].astype(jnp.float32)
    lam = jnp.exp(jnp.sum(lq1 * lk1)) - jnp.exp(jnp.sum(lq2 * lk2)) + lam_init
    if k_past is None:
        o = attn_prompt(q, k, v, lam)
    else:
        k_all = jnp.concatenate([k_past.reshape(b, p0, N_HEADS, 2, HEAD_DIM).astype(k.dtype), k], axis=1)
        v_all = jnp.concatenate([v_past.astype(v.dtype), v], axis=1)
        k_pos = jnp.arange(p0 + s, dtype=jnp.int32)
        o = diff_attend(q, k_all, v_all, pos, k_pos, lam)
    o = rms_norm(o, prm['attn_norm_g']) * (1.0 - lam_init)
    attn_out = o.reshape(b, s, ATTN_W) @ prm['w_attn_out']

    g = jax.nn.sigmoid(u[..., OFF_G:] + prm['b_gate'])
    mix = (g[..., :D_MODEL] * conv_out + g[..., D_MODEL:] * attn_out) @ prm['w_out']
    x = layer_norm(ALPHA * x + mix, prm['ln1_g'], prm['ln1_b'])

    f = jnp.square(jax.nn.relu(x @ prm['w_up'] + prm['b_up'])) @ prm['w_down'] + prm['b_down']
    x = layer_norm(ALPHA * x + f, prm['ln2_g'], prm['ln2_b'])
    return x, k.reshape(b, s, N_HEADS, 2 * HEAD_DIM), v, new_conv


def setup_inputs(seed: int = 0) -> dict:
    key = jax.random.key(seed)
    ks = jax.random.split(key, 32)
    nrm = lambda k, shape, sc: jax.random.normal(k, shape, jnp.float32) * sc
    return {
        'x_prompt': nrm(ks[0], (BATCH, SEQ, D_MODEL), 1.0),
        'x_sample': nrm(ks[1], (DEC_BATCH, DEC_SEQ, D_MODEL), 1.0),
        'cache_k': nrm(ks[2], (DEPTH, DEC_BATCH, PAST_LEN, N_HEADS, 2 * HEAD_DIM), 1.0),
        'cache_v': nrm(ks[3], (DEPTH, DEC_BATCH, PAST_LEN, N_HEADS, V_DIM), 1.0),
        'state_conv': nrm(ks[4], (DEPTH, DEC_BATCH, CONV_WIDTH - 1, D_CONV), 0.5),
        'ln_in_g': 1.0 + nrm(ks[5], (D_MODEL,), 0.02),
        'ln_in_b': nrm(ks[6], (D_MODEL,), 0.02),
        'w_in': nrm(ks[7], (DEPTH, D_MODEL, IN_COLS), D_MODEL ** -0.5),
        'b_gate': nrm(ks[8], (DEPTH, 2 * D_MODEL), 0.02),
        'conv_dw_w': nrm(ks[9], (DEPTH, CONV_WIDTH, D_CONV), CONV_WIDTH ** -0.5),
        'conv_dw_b': nrm(ks[10], (DEPTH, D_CONV), 0.02),
        'conv_ln_g': 1.0 + nrm(ks[11], (DEPTH, D_CONV), 0.02),
        'conv_ln_b': nrm(ks[12], (DEPTH, D_CONV), 0.02),
        'w_conv_out': nrm(ks[13], (DEPTH, D_CONV, D_MODEL), BETA * D_CONV ** -0.5),
        'lambda_q1': nrm(ks[14], (DEPTH, HEAD_DIM), 0.1),
        'lambda_k1': nrm(ks[15], (DEPTH, HEAD_DIM), 0.1),
        'lambda_q2': nrm(ks[16], (DEPTH, HEAD_DIM), 0.1),
        'lambda_k2': nrm(ks[17], (DEPTH, HEAD_DIM), 0.1),
        'attn_norm_g': 1.0 + nrm(ks[18], (DEPTH, V_DIM), 0.02),
        'w_attn_out': nrm(ks[19], (DEPTH, ATTN_W, D_MODEL), BETA * ATTN_W ** -0.5),
        'w_out': nrm(ks[20], (DEPTH, D_MODEL, D_MODEL), BETA * D_MODEL ** -0.5),
        'ln1_g': 1.0 + nrm(ks[21], (DEPTH, D_MODEL), 0.02),
        'ln1_b': nrm(ks[22], (DEPTH, D_MODEL), 0.02),
        'w_up': nrm(ks[23], (DEPTH, D_MODEL, D_FF), D_MODEL ** -0.5),
        'b_up': nrm(ks[24], (DEPTH, D_FF), 0.02),
        'w_down': nrm(ks[25], (DEPTH, D_FF, D_MODEL), BETA * D_FF ** -0.5),
        'b_down': nrm(ks[26], (DEPTH, D_MODEL), 0.02),
        'ln2_g': 1.0 + nrm(ks[27], (DEPTH, D_MODEL), 0.02),
        'ln2_b': nrm(ks[28], (DEPTH, D_MODEL), 0.02),
    }


def reference(x_prompt, x_sample, cache_k, cache_v, state_conv, ln_in_g, ln_in_b, w_in, b_gate,
              conv_dw_w, conv_dw_b, conv_ln_g, conv_ln_b, w_conv_out, lambda_q1, lambda_k1,
              lambda_q2, lambda_k2, attn_norm_g, w_attn_out, w_out, ln1_g, ln1_b, w_up, b_up,
              w_down, b_down, ln2_g, ln2_b):
    xp = layer_norm(x_prompt, ln_in_g, ln_in_b)
    xs = layer_norm(x_sample, ln_in_g, ln_in_b)
    kp_l, vp_l, cp_l, ks_l, vs_l, cs_l = [], [], [], [], [], []
    for l in range(DEPTH):
        prm = {
            'w_in': w_in[l], 'b_gate': b_gate[l], 'conv_dw_w': conv_dw_w[l], 'conv_dw_b': conv_dw_b[l],
            'conv_ln_g': conv_ln_g[l], 'conv_ln_b': conv_ln_b[l], 'w_conv_out': w_conv_out[l],
            'lambda_q1': lambda_q1[l], 'lambda_k1': lambda_k1[l], 'lambda_q2': lambda_q2[l],
            'lambda_k2': lambda_k2[l], 'attn_norm_g': attn_norm_g[l], 'w_attn_out': w_attn_out[l],
            'w_out': w_out[l], 'ln1_g': ln1_g[l], 'ln1_b': ln1_b[l], 'w_up': w_up[l], 'b_up': b_up[l],
            'w_down': w_down[l], 'b_down': b_down[l], 'ln2_g': ln2_g[l], 'ln2_b': ln2_b[l],
        }
        lam_init = 0.8 - 0.6 * math.exp(-0.3 * l)
        zero_hist = jnp.zeros((xp.shape[0], CONV_WIDTH - 1, D_CONV), xp.dtype)
        xp, kp, vp, cp = layer_step(xp, zero_hist, None, None, prm, lam_init)
        xs, ksn, vsn, csn = layer_step(xs, state_conv[l], cache_k[l], cache_v[l], prm, lam_init)
        kp_l.append(kp); vp_l.append(vp); cp_l.append(cp)
        ks_l.append(ksn); vs_l.append(vsn); cs_l.append(csn)
    new_k_prompt = jnp.stack(kp_l)
    new_v_prompt = jnp.stack(vp_l)
    new_conv_prompt = jnp.stack(cp_l)
    new_k_sample = jnp.stack(ks_l)
    new_v_sample = jnp.stack(vs_l)
    new_conv_sample = jnp.stack(cs_l)
    return (xp, xs, new_k_prompt, new_v_prompt, new_conv_prompt, new_k_sample, new_v_sample, new_conv_sample)
```

```python
import math
from functools import partial
import numpy as np
import concourse.bass as bass
import concourse.mybir as mybir
from concourse.bass_utils import run_bass_kernel_spmd

F32 = mybir.dt.float32
BF16 = mybir.dt.bfloat16
AF = mybir.ActivationFunctionType
ALU = mybir.AluOpType

D = 1024
SEQ = 8192
DEPTH = 2
NCORES = 8
PAST = 2048
DSEQ = 64
H = 8
CW = 31
EPS = 1e-5
ALPHA = (2 * DEPTH) ** 0.25
THETA = 500000.0
NGRAN = 36
KCHUNK = 1024
NWBUF = 5
G_A0, G_B0, G_A1, G_B1, G_Q0, G_Q1, G_K0, G_K1, G_V0, G_V1 = range(10)
G_GC0, G_GA0, G_CO0, G_AO0, G_GC1, G_GA1, G_CO1, G_AO1, G_WO0, G_WO1 = range(10, 20)
G_UP = 20
G_DN = 28
PP_BG, PP_CW, PP_CB, PP_LG, PP_LB, PP_BU = 0, 16, 16 + 248, 16 + 256, 16 + 264, 16 + 272
PP_L = 16 + 272 + 32
RV_LNIN_G, RV_LNIN_B = 0, 1024
RV_L0 = 2048
RV_LN1G, RV_LN1B, RV_LN2G, RV_LN2B, RV_BD, RV_ANG, RV_LQ1, RV_LK1, RV_LQ2, RV_LK2 = (
    0, 1024, 2048, 3072, 4096, 5120, 5248, 5312, 5376, 5440)
RV_LSZ = 5504
N_PROMPT_TILES = SEQ // 512


class Buf:
    __slots__ = ("name", "lw", "rd", "alias")

    def __init__(self, name):
        self.name = name
        self.lw = None
        self.rd = []
        self.alias = ()


class Sem:
    __slots__ = ("h", "count")

    def __init__(self, h):
        self.h = h
        self.count = 0


class Op:
    __slots__ = ("eng", "fn", "deps", "sig", "sem", "sigval", "dma")

    def __init__(self, eng, fn, dma):
        self.eng = eng
        self.fn = fn
        self.deps = ()
        self.sig = False
        self.sem = None
        self.sigval = 0
        self.dma = dma


class Prog:
    def __init__(self, nc):
        self.nc = nc
        self.ops = []
        self.eng_obj = {"pe": nc.tensor, "act": nc.scalar, "dve": nc.vector, "pool": nc.gpsimd, "sp": nc.sync}
        self.eng_sem = {k: Sem(nc.alloc_semaphore("s_" + k)) for k in self.eng_obj}
        self.dma_sems = {"sp": [Sem(nc.alloc_semaphore(f"d_sp{i}")) for i in range(16)],
                         "pool": [Sem(nc.alloc_semaphore(f"d_po{i}")) for i in range(12)],
                         "act": [Sem(nc.alloc_semaphore(f"d_ac{i}")) for i in range(6)]}
        self.dma_rr = {"sp": 0, "pool": 0, "act": 0}

    def _record(self, o, r, w, nosync):
        deps = set()
        eng = o.eng
        for b in r:
            lw = b.lw
            if lw is not None:
                if nosync and (not lw.dma) and lw.eng == eng and not o.dma:
                    continue
                deps.add(lw)
        wexp = list(w)
        for b in w:
            if b.alias:
                wexp.extend(b.alias)
        for b in wexp:
            lw = b.lw
            if lw is not None and (o.dma or lw.dma or lw.eng != eng):
                deps.add(lw)
            for rd in b.rd:
                if o.dma or rd.dma or rd.eng != eng:
                    deps.add(rd)
        deps.discard(o)
        for d in deps:
            d.sig = True
        o.deps = tuple(deps)
        for b in wexp:
            b.lw = o
            b.rd = []
        for b in r:
            if b.lw is o:
                continue
            if not o.dma:
                b.rd = [x for x in b.rd if x.dma or x.eng != eng]
            b.rd.append(o)
        self.ops.append(o)
        return o

    def op(self, eng, fn, r=(), w=(), nosync=False):
        return self._record(Op(eng, fn, False), r, w, nosync)

    def dma(self, q, out, in_, r=(), w=(), **kw):
        o = Op(q, partial(self.eng_obj[q].dma_start, out=out, in_=in_, **kw), True)
        pool = self.dma_sems[q]
        o.sem = pool[self.dma_rr[q] % len(pool)]
        self.dma_rr[q] += 1
        return self._record(o, r, w, False)

    def emit(self):
        seen = {k: {} for k in self.eng_obj}
        for o in self.ops:
            eobj = self.eng_obj[o.eng]
            sn = seen[o.eng]
            need = {}
            for d in o.deps:
                s = d.sem
                if need.get(s, 0) < d.sigval:
                    need[s] = d.sigval
            if o.dma:
                s = o.sem
                if s.count > need.get(s, 0):
                    need[s] = s.count
            for s, v in need.items():
                if sn.get(s, 0) >= v:
                    continue
                eobj.wait_ge(s.h, v)
                sn[s] = v
            ins = o.fn()
            if o.dma:
                s = o.sem
                s.count += 16
                ins.then_inc(s.h, 16)
                o.sigval = s.count
            elif o.sig:
                s = self.eng_sem[o.eng]
                s.count += 1
                ins.then_inc(s.h, 1)
                o.sem = s
                o.sigval = s.count
            o.fn = None
        sp = self.nc.sync
        for q in self.dma_sems:
            for s in self.dma_sems[q]:
                if s.count:
                    sp.wait_ge(s.h, s.count)


def build_program(n_prompt_tiles=N_PROMPT_TILES, do_sample=True, dumps=None):
    nc = bass.Bass("TRN2", target_bir_lowering=False)
    P = Prog(nc)

    def din(name, shape, dt=F32):
        return nc.dram_tensor(name, list(shape), dt, kind="ExternalInput").ap()

    def dout(name, shape, dt=F32):
        return nc.dram_tensor(name, list(shape), dt, kind="ExternalOutput").ap()

    def dscr(name, shape, dt):
        return nc.dram_tensor(name, list(shape), dt, kind="Internal").ap()

    def dump(name, ap, bufs):
        if dumps is None or name in dumps:
            return
        dumps.add(name)
        d = nc.dram_tensor("dbg_" + name, list(ap.shape), ap.dtype, kind="ExternalOutput").ap()
        P.dma("sp", d, ap, r=bufs)

    xp = din("xp", [SEQ, D])
    xs = din("xs", [128, D])
    ckT = din("ckT", [DEPTH, 2, 128, H, PAST])
    cv = din("cv", [DEPTH, 2, PAST, H, 128])
    scT = din("scT", [DEPTH, 2, 128, 8, 30])
    wg = din("wg", [DEPTH, NGRAN, 128, 8 * 512])
    pp_d = din("pp", [128, DEPTH * PP_L])
    rv_d = din("rv", [RV_L0 + DEPTH * RV_LSZ])
    cosx = din("cosx", [SEQ + DSEQ, 256])
    sinx = din("sinx", [SEQ + DSEQ, 256])
    ident_d = din("ident", [128, 128])

    y_p = dout("y_p", [SEQ, D])
    y_s = dout("y_s", [128, D])
    nk_p = dout("nk_p", [DEPTH, SEQ, D])
    nv_p = dout("nv_p", [DEPTH, SEQ, D])
    ncv_p = dout("ncv_p", [DEPTH, 30, D])
    nk_s = dout("nk_s", [DEPTH, 128, D])
    nv_s = dout("nv_s", [DEPTH, 128, D])
    ncv_s = dout("ncv_s", [DEPTH, 2, 30, D])

    wbf = dscr("wbf", [DEPTH, NGRAN, 128, 8 * 512], BF16)
    kTs = dscr("kTs", [DEPTH, H, 128, SEQ], BF16)
    vsc = dscr("vsc", [DEPTH, SEQ, H, 128], BF16)
    B_wbf = [[Buf(f"wbf{l}_{g}") for g in range(NGRAN)] for l in range(DEPTH)]
    B_kTs = [[Buf(f"kTs{l}_{t}") for t in range(N_PROMPT_TILES)] for l in range(DEPTH)]
    B_vsc = [[Buf(f"vsc{l}_{t}") for t in range(N_PROMPT_TILES * 4)] for l in range(DEPTH)]

    def sb(name, shape, dt):
        return nc.alloc_sbuf_tensor(name, list(shape), dt)

    XR = sb("XR", [128, 4, D], F32)
    B_XR = [[Buf(f"XR{s}a"), Buf(f"XR{s}b")] for s in range(4)]
    XT = sb("XT", [128, 8, 512], BF16)
    B_XT = [Buf(f"XT{s}") for s in range(4)]
    CONVMEM = sb("CONVMEM", [128, 8 * 542 + 8 * 512], F32)
    GLU = CONVMEM[:, 0:8 * 542].rearrange("p (a b) -> p a b", a=8)
    B_GLU = [Buf(f"GLU{j}") for j in range(8)]
    ACC = CONVMEM[:, 8 * 542:8 * 542 + 8 * 512].rearrange("p (a b) -> p a b", a=8)
    B_ACC = [Buf(f"ACC{j}") for j in range(8)]
    HT = CONVMEM[:, 0:8192].bitcast(BF16).rearrange("p (a b) -> p a b", a=32)
    TMPF = sb("TMPF", [128, 3, 512], F32)
    B_TMPF = [Buf(f"TMPF{i}") for i in range(3)]
    QKVs = [sb(f"QKV{i}", [128, 2048], F32) for i in range(2)]
    B_QKVs = [Buf(f"QKV{i}") for i in range(2)]
    QKV, B_QKV = QKVs[0], B_QKVs[0]
    QTZ = sb("QTZ", [128, 2, 8, 512], BF16)
    B_QT = [Buf(f"QT{s}") for s in range(4)]
    KTC = sb("KTC", [128, 8, 512], BF16)
    B_KTC = [Buf(f"KTC{s}") for s in range(4)]
    VC = sb("VC", [128, 4, 8, 130], BF16)
    B_VC = [Buf(f"VC{s}") for s in range(4)]
    NBLK = KCHUNK // 128
    KVMEM = sb("KVMEM", [128, 2 * KCHUNK + 2 * NBLK * 130], BF16)
    KCH = [KVMEM[:, i * KCHUNK:(i + 1) * KCHUNK] for i in range(2)]
    VCH = [KVMEM[:, 2 * KCHUNK + i * NBLK * 130: 2 * KCHUNK + (i + 1) * NBLK * 130].rearrange("p (a b) -> p a b", a=NBLK)
           for i in range(2)]
    MT = KVMEM[:, 0:4096].rearrange("p (a b) -> p a b", a=8)
    B_KCH = [Buf(f"KCH{i}") for i in range(2)]
    B_VCH = [Buf(f"VCH{i}") for i in range(2)]
    PTMEM = sb("PTMEM", [128, 4 * 512], BF16)
    PT = [PTMEM[:, i * 512:(i + 1) * 512] for i in range(4)]
    B_PT = [Buf(f"PT{i}") for i in range(4)]
    RT = PTMEM[:, :].bitcast(F32).rearrange("p (a b) -> p a b", a=4)
    OFMEM = sb("OFMEM", [128, 512], F32)
    OF = OFMEM[:, :].rearrange("p (a b) -> p a b", a=4)
    B_OF = Buf("OF")
    CS = OFMEM[:, :].rearrange("p (a b) -> p a b", a=2)
    TMPO = sb("TMPO", [128, 2, 128], F32)
    B_TMPO = [Buf("TMPO0"), Buf("TMPO1")]
    SM = sb("SM", [128, 64], F32)
    B_SM = Buf("SM")
    OBMEM = sb("OBMEM", [128, 4 * D], BF16)
    OB = OBMEM[:, :].rearrange("p (a b) -> p a b", a=4)
    HCT = OBMEM[:, :].rearrange("p (a b) -> p a b", a=8)
    B_OB = [Buf(f"OB{s}") for s in range(4)]
    XBs = [sb(f"XB{i}", [128, D], BF16) for i in range(2)]
    B_XBs = [Buf(f"XB{i}") for i in range(2)]
    GB = sb("GB", [128, 5, D], F32)
    B_GB = [Buf(f"GB{i}") for i in range(5)]
    WB = [sb(f"WB{i}", [128, 8, 512], BF16) for i in range(NWBUF)]
    B_WB = [Buf(f"WB{i}") for i in range(NWBUF)]
    B_CS = Buf("CS")
    B_RT = Buf("RT")
    PPt = sb("PPt", [128, DEPTH * PP_L], F32)
    B_PP = Buf("PP")
    HIST = sb("HIST", [128, DEPTH, 8, 30], F32)
    B_HIST = [Buf(f"HIST{l}") for l in range(DEPTH)]
    IDF = sb("IDF", [128, 128], F32)
    IDB = sb("IDB", [128, 128], BF16)
    B_ID = Buf("ID")
    ONES = sb("ONES", [128, 128], F32)
    MH = sb("MH", [128, 4], F32)
    B_CONST = Buf("CONST")
    ROWS = QKV[0:1, :].rearrange("p (a b) -> p a b", a=4)
    B_ROWS = B_QKV
    STAT = sb("STAT", [128, 4, 2, 6], F32)
    MV = sb("MV", [128, 4, 4], F32)
    B_STAT = Buf("STAT")
    B_STATs = [Buf(f"STAT{i}") for i in range(4)]
    LAMT = sb("LAMT", [128, DEPTH, 4], F32)
    B_LAM = Buf("LAM")
    LTMP = sb("LTMP", [128, 4, 64], F32)
    B_LTMP = Buf("LTMP")
    GN = sb("GN", [128, DEPTH, 128], F32)
    B_GN = Buf("GN")
    B_HT = [Buf(f"HT{f}") for f in range(32)]
    conv_bufs = tuple(B_GLU + B_ACC)
    for b in B_HT:
        b.alias = conv_bufs
    for b in conv_bufs:
        b.alias = tuple(B_HT)
    B_MT = [Buf(f"MT{s}") for s in range(4)]
    kv_bufs = tuple(B_KCH + B_VCH)
    for b in B_MT:
        b.alias = kv_bufs
    for b in kv_bufs:
        b.alias = tuple(B_MT)
    B_RT.alias = tuple(B_PT)
    for b in B_PT:
        b.alias = (B_RT,)
    B_CS.alias = (B_OF,)
    B_OF.alias = (B_CS,)
    EPSC = sb("EPSC", [128, 1], F32)
    print("SBUF bytes remaining:", nc.sbuf_bytes_remaining)
    OT = KTC
    B_OT = B_KTC
    B_HCT = [Buf(f"HCT{j}") for j in range(8)]
    for b in B_HCT:
        b.alias = tuple(B_OB)
    for b in B_OB:
        b.alias = tuple(B_HCT)

    PSALL = nc.alloc_psum_tensor("psall", [128, 8 * 512], F32)
    PB = [PSALL[:, i * 512:(i + 1) * 512] for i in range(8)]
    B_PB = [Buf(f"pb{i}") for i in range(8)]
    rr = {"bank": 0, "tmpf": 0, "xb": 0}

    def next_xb():
        i = rr["xb"] % 2
        rr["xb"] += 1
        return XBs[i], B_XBs[i]

    def next_bank():
        i = rr["bank"] % 8
        rr["bank"] += 1
        return i

    def next_tmpf():
        i = rr["tmpf"] % 3
        rr["tmpf"] += 1
        return i

    T_ = nc.tensor
    V_ = nc.vector
    A_ = nc.scalar
    G_ = nc.gpsimd

    P.dma("sp", PPt[:], pp_d[:, :], w=[B_PP])
    P.dma("sp", IDF[:], ident_d[:, :], w=[B_ID])
    P.op("dve", partial(V_.tensor_copy, out=IDB[:], in_=IDF[:]), r=[B_ID], w=[B_ID])
    P.op("dve", partial(V_.memset, ONES[:], 1.0), w=[B_CONST])
    P.op("dve", partial(V_.memset, MH[:], -0.5), w=[B_CONST])
    P.op("dve", partial(V_.memset, EPSC[:], EPS), w=[B_CONST])
    P.op("pool", partial(G_.memset, QTZ[:].rearrange("p a b c -> p (a b c)"), 0.0), w=B_QT)
    P.op("dve", partial(V_.memset, VC[:, :, :, 128:130], 1.0), w=B_VC)
    for l in range(DEPTH):
        P.op("dve", partial(V_.memset, HIST[:, l], 0.0), w=[B_HIST[l]])
    for l in range(DEPTH):
        lam_init = 0.8 - 0.6 * math.exp(-0.3 * l)
        base = RV_L0 + l * RV_LSZ
        for i, off in enumerate((RV_LQ1, RV_LK1, RV_LQ2, RV_LK2)):
            P.dma("sp", LTMP[:, i, :], rv_d[base + off: base + off + 64].partition_broadcast(128), w=[B_LTMP])
        P.dma("sp", GN[:, l, :], rv_d[base + RV_ANG: base + RV_ANG + 128].partition_broadcast(128), w=[B_GN])
        P.op("dve", partial(V_.tensor_tensor, out=TMPO[:, 1, 0:64], in0=LTMP[:, 0, :], in1=LTMP[:, 1, :], op=ALU.mult),
             r=[B_LTMP], w=[B_TMPO[1]])
        P.op("dve", partial(V_.reduce_sum, out=LAMT[:, l, 2:3], in_=TMPO[:, 1, 0:64], axis=mybir.AxisListType.X),
             r=[B_TMPO[1]], w=[B_LAM])
        P.op("dve", partial(V_.tensor_tensor, out=TMPO[:, 1, 0:64], in0=LTMP[:, 2, :], in1=LTMP[:, 3, :], op=ALU.mult),
             r=[B_LTMP], w=[B_TMPO[1]])
        P.op("dve", partial(V_.reduce_sum, out=LAMT[:, l, 3:4], in_=TMPO[:, 1, 0:64], axis=mybir.AxisListType.X),
             r=[B_TMPO[1]], w=[B_LAM])
        P.op("act", partial(A_.activation, out=LAMT[:, l, 2:4], in_=LAMT[:, l, 2:4], func=AF.Exp), r=[B_LAM], w=[B_LAM])
        P.op("dve", partial(V_.tensor_tensor, out=LAMT[:, l, 0:1], in0=LAMT[:, l, 2:3], in1=LAMT[:, l, 3:4], op=ALU.subtract),
             r=[B_LAM], w=[B_LAM])
        P.op("dve", partial(V_.tensor_scalar, out=LAMT[:, l, 0:1], in0=LAMT[:, l, 0:1], scalar1=lam_init, scalar2=None, op0=ALU.add),
             r=[B_LAM], w=[B_LAM])
        P.op("dve", partial(V_.tensor_scalar, out=LAMT[:, l, 1:2], in0=LAMT[:, l, 0:1], scalar1=-1.0, scalar2=None, op0=ALU.mult),
             r=[B_LAM], w=[B_LAM])
        P.op("dve", partial(V_.tensor_scalar, out=GN[:, l, :], in0=GN[:, l, :], scalar1=1.0 - lam_init, scalar2=None, op0=ALU.mult),
             r=[B_GN], w=[B_GN])

    ws = {"conv_next": [0, 0], "load_next": 0, "seq": [], "first_pass": True}
    tile_seq = [(l, g) for l in range(DEPTH) for g in range(NGRAN)]
    n_tiles_total = (1 if do_sample else 0) + n_prompt_tiles
    full_seq = tile_seq * n_tiles_total
    wstate = {"loaded": 0, "conv": 0, "cons": 0, "rel": 0}

    def w_convert(upto):
        while wstate["conv"] < min(upto, len(tile_seq)):
            l, g = tile_seq[wstate["conv"]]
            P.dma("pool", wbf[l, g].rearrange("p (a b) -> p a b", a=8), wg[l, g].rearrange("p (a b) -> p a b", a=8), w=[B_wbf[l][g]])
            wstate["conv"] += 1

    def w_prefetch():
        lim = min(wstate["rel"] + NWBUF, len(full_seq))
        w_convert(lim + 3)
        while wstate["loaded"] < lim:
            j = wstate["loaded"]
            ll, gg = full_seq[j]
            P.dma("sp", WB[j % NWBUF][:].rearrange("p a b -> p (a b)"), wbf[ll, gg],
                  r=[B_wbf[ll][gg]], w=[B_WB[j % NWBUF]])
            wstate["loaded"] += 1

    def w_get(l, g):
        i = wstate["cons"]
        assert full_seq[i] == (l, g), (full_seq[i], l, g)
        w_prefetch()
        assert wstate["loaded"] > i, "weight granule held too long"
        wstate["cons"] += 1
        return WB[i % NWBUF], B_WB[i % NWBUF]

    def w_done(n=1):
        wstate["rel"] += n
        assert wstate["rel"] <= wstate["cons"]
        w_prefetch()

    w_convert(NWBUF + 3)

    def pp(l, off, n=1):
        return PPt[:, l * PP_L + off: l * PP_L + off + n]

    def layer_norm_one(s, gi, bi, xr=None, xb=None):
        if xr is None:
            xr = XR[:, s, :]
            xb = B_XR[s]
        for i in range(2):
            P.op("dve", partial(V_.bn_stats, out=STAT[:, s, i, :], in_=xr[:, i * 512:(i + 1) * 512]),
                 r=[*xb], w=[B_STATs[s]])
        P.op("dve", partial(V_.bn_aggr, out=MV[:, s, 0:2], in_=STAT[:, s].rearrange("p a b -> p (a b)")),
             r=[B_STATs[s]], w=[B_STATs[s]])
        P.op("act", partial(A_.activation, out=MV[:, s, 2:3], in_=MV[:, s, 1:2], func=AF.Sqrt, bias=EPSC[:, :]),
             r=[B_STATs[s], B_CONST], w=[B_STATs[s]])
        P.op("dve", partial(V_.reciprocal, out=MV[:, s, 2:3], in_=MV[:, s, 2:3]), r=[B_STATs[s]], w=[B_STATs[s]])
        P.op("dve", partial(V_.scalar_tensor_tensor, out=MV[:, s, 3:4], in0=MV[:, s, 0:1], scalar=-1.0, in1=MV[:, s, 2:3],
                            op0=ALU.mult, op1=ALU.mult), r=[B_STATs[s]], w=[B_STATs[s]])
        P.op("act", partial(A_.activation, out=xr, in_=xr, func=AF.Identity, scale=MV[:, s, 2:3], bias=MV[:, s, 3:4]),
             r=[*xb, B_STATs[s]], w=[*xb])
        P.op("pool", partial(G_.tensor_tensor, out=xr, in0=xr, in1=GB[:, gi, :], op=ALU.mult),
             r=[*xb, B_GB[gi]], w=[*xb])
        P.op("dve", partial(V_.tensor_tensor, out=xr[:, 0:576], in0=xr[:, 0:576], in1=GB[:, bi, 0:576], op=ALU.add),
             r=[xb[0], B_GB[bi]], w=[xb[0]])
        P.op("pool", partial(G_.tensor_tensor, out=xr[:, 576:1024], in0=xr[:, 576:1024], in1=GB[:, bi, 576:1024], op=ALU.add),
             r=[xb[1], B_GB[bi]], w=[xb[1]])

    def make_xT(s):
        XB, B_XB = next_xb()
        P.op("act", partial(A_.copy, out=XB[:], in_=XR[:, s, :]), r=[*B_XR[s]], w=[B_XB])
        bk = next_bank()
        pv = PB[bk][:].bitcast(BF16).rearrange("p (a b) -> p a b", a=8)
        for j in range(8):
            P.op("pe", partial(T_.transpose, out=pv[:, j, :], in_=XB[:, j * 128:(j + 1) * 128], identity=IDB[:]),
                 r=[B_XB, B_ID], w=[B_PB[bk]])
        P.op("act", partial(A_.copy, out=XT[:, :, s * 128:(s + 1) * 128], in_=pv), r=[B_PB[bk]], w=[B_XT[s]])

    def load_gb(l):
        base = RV_L0 + l * RV_LSZ
        for i, off in enumerate((RV_LN1G, RV_LN1B, RV_LN2G, RV_LN2B, RV_BD)):
            P.dma("sp", GB[:, i, :], rv_d[base + off: base + off + D].partition_broadcast(128), w=[B_GB[i]])

    def qkv_view(s):
        return QKVs[s // 2][:, (s % 2) * 1024:(s % 2 + 1) * 1024], [B_QKVs[s // 2], B_QKVs[s // 2]]

    def run_tile(kind, t, prefetched=False, nxt=None):
        nsub = 4 if kind == "p" else 1
        T = 128 * nsub
        nseg = 1 if kind == "p" else 2
        segn = T // nseg
        segw = 30 + segn
        row0 = t * 512 if kind == "p" else 0
        x_src = xp if kind == "p" else xs
        SS = list(range(nsub))

        def gluv(jc, off, n):
            if nseg == 1:
                return GLU[:, jc, off:off + n]
            return GLU[:, jc, 0:nseg * segw].rearrange("p (s w) -> p s w", s=nseg)[:, :, off:off + n]

        def segview(ap2d):
            if nseg == 1:
                return ap2d
            return ap2d.rearrange("p (s w) -> p s w", s=nseg)

        if prefetched:
            for s in SS:
                qv_, qb_ = qkv_view(s)
                P.op("act", partial(A_.copy, out=XR[:, s, :], in_=qv_), r=qb_[:1], w=[*B_XR[s]])
        else:
            P.dma("sp", GB[:, 2, :], rv_d[RV_LNIN_G:RV_LNIN_G + D].partition_broadcast(128), w=[B_GB[2]])
            P.dma("sp", GB[:, 3, :], rv_d[RV_LNIN_B:RV_LNIN_B + D].partition_broadcast(128), w=[B_GB[3]])
            for s in SS:
                P.dma("sp", XR[:, s, :], x_src[row0 + s * 128: row0 + (s + 1) * 128, :], w=[*B_XR[s]])
            for s in SS:
                layer_norm_one(s, 2, 3)

        for l in range(DEPTH):
            lam_init = 0.8 - 0.6 * math.exp(-0.3 * l)
            last_layer = (l == DEPTH - 1)
            nk_dst = nk_p if kind == "p" else nk_s
            nv_dst = nv_p if kind == "p" else nv_s
            load_gb(l)
            units = []
            loads = []
            for hh in range(8):
                hunits = []
                if kind == "p":
                    npast = row0
                    for k0 in range(0, npast, KCHUNK):
                        n = min(KCHUNK, npast - k0)
                        li = len(loads)
                        par = li % 2
                        tlist = list(range(k0 // 512, (k0 + n + 511) // 512))

                        def ld(par=par, k0=k0, n=n, hh=hh, tlist=tlist):
                            P.dma("sp", KCH[par][:, 0:n], kTs[l, hh, :, k0:k0 + n],
                                  r=[B_kTs[l][tt] for tt in tlist], w=[B_KCH[par]])
                            P.dma("sp", VCH[par][:, 0:n // 128, 0:128],
                                  vsc[l, k0:k0 + n, hh, :].rearrange("(b p) e -> p b e", p=128),
                                  r=[B_vsc[l][bb] for bb in range(k0 // 128, (k0 + n) // 128)], w=[B_VCH[par]])
                        loads.append(ld)
                        for j in range(n // 128):
                            for c in range(2):
                                hunits.append(dict(h=hh, c=c, load=li, kp0=0, nk=128,
                                                   lhsT=KCH[par][:, j * 128:(j + 1) * 128],
                                                   rk=[B_KCH[par]], vrhs=VCH[par][:, j, :], rv=[B_VCH[par]],
                                                   q0=0, q1=512, pv=[(s, s * 128, 128, 0) for s in range(4)], mask=None))
                    for kb in range(4):
                        for c in range(2):
                            hunits.append(dict(h=hh, c=c, load=None, kp0=0, nk=128,
                                               lhsT=KTC[:, hh, kb * 128:(kb + 1) * 128],
                                               rk=[B_KTC[kb]], vrhs=VC[:, kb, hh, :], rv=[B_VC[kb]],
                                               q0=kb * 128, q1=512, pv=[(s, s * 128, 128, 0) for s in range(kb, 4)],
                                               mask=(64, 128, kb * 128, kb * 128 + 64)))
                else:
                    for b in range(2):
                        for k0 in range(0, PAST, KCHUNK):
                            n = KCHUNK
                            li = len(loads)
                            par = li % 2

                            def ld(par=par, k0=k0, n=n, hh=hh, b=b):
                                P.dma("pool", KCH[par][:, 0:n], ckT[l, b, :, hh, k0:k0 + n], w=[B_KCH[par]])
                                P.dma("pool", VCH[par][:, 0:n // 128, 0:128],
                                      cv[l, b, k0:k0 + n, hh, :].rearrange("(b p) e -> p b e", p=128), w=[B_VCH[par]])
                            loads.append(ld)
                            for j in range(n // 128):
                                for c in range(2):
                                    hunits.append(dict(h=hh, c=c, load=li, kp0=0, nk=128,
                                                       lhsT=KCH[par][:, j * 128:(j + 1) * 128],
                                                       rk=[B_KCH[par]], vrhs=VCH[par][:, j, :], rv=[B_VCH[par]],
                                                       q0=b * 64, q1=b * 64 + 64, pv=[(0, 0, 128, 0)], mask=None,
                                                       ptc0=b * 64, zero_other=(k0 == 0 and j == 0 and c == 0), b=b))
                        for c in range(2):
                            hunits.append(dict(h=hh, c=c, load=None, kp0=b * 64, nk=64,
                                               lhsT=KTC[:, hh, b * 64:(b + 1) * 64],
                                               rk=[B_KTC[0]], vrhs=VC[b * 64:(b + 1) * 64, 0, hh, :], rv=[B_VC[0]],
                                               q0=b * 64, q1=b * 64 + 64, pv=[(0, 0, 128, 0)], mask=None,
                                               ptc0=b * 64))
                hunits[-1]["last_of_head"] = True
                seen_start = set()
                for u in hunits:
                    fl = []
                    for (s, qa, qn, op0) in u["pv"]:
                        key = (s, op0)
                        fl.append(key not in seen_start)
                        seen_start.add(key)
                    u["start"] = fl
                units.extend(hunits)
            nun = len(units)
            conv_per_unit = -(-(8 * CW + 1) // nun)
            loads_done = [0]

            def ensure_load(li):
                while loads_done[0] <= li and loads_done[0] < len(loads):
                    loads[loads_done[0]]()
                    loads_done[0] += 1

            for i in range(2):
                P.op("dve", partial(V_.memset, VCH[i][:, :, 128:130], 1.0), w=[B_VCH[i]])
            if loads:
                ensure_load(0)
            if len(loads) > 1:
                ensure_load(1)
            for s in SS:
                make_xT(s)
            if kind == "p":
                P.op("pool", partial(G_.tensor_copy, out=GLU[:, :, 0:30], in_=HIST[:, l]), r=[B_HIST[l]], w=B_GLU)
            else:
                for b in range(2):
                    P.dma("sp", GLU[:, :, b * segw: b * segw + 30], scT[l, b], w=B_GLU)
            for half in range(2):
                wa, bwa = w_get(l, G_A0 + 2 * half)
                wb_, bwb = w_get(l, G_B0 + 2 * half)
                for j in range(4):
                    jc = half * 4 + j
                    ba = next_bank()
                    bb = next_bank()
                    for kc in range(8):
                        P.op("pe", partial(T_.matmul, PB[ba][:, 0:T], lhsT=wa[:, kc, j * 128:(j + 1) * 128], rhs=XT[:, kc, 0:T],
                                           start=(kc == 0), stop=(kc == 7)), r=[bwa] + B_XT[:nsub], w=[B_PB[ba]])
                    for kc in range(8):
                        P.op("pe", partial(T_.matmul, PB[bb][:, 0:T], lhsT=wb_[:, kc, j * 128:(j + 1) * 128], rhs=XT[:, kc, 0:T],
                                           start=(kc == 0), stop=(kc == 7)), r=[bwb] + B_XT[:nsub], w=[B_PB[bb]])
                    ti = next_tmpf()
                    P.op("act", partial(A_.activation, out=TMPF[:, ti, 0:T], in_=PB[bb][:, 0:T], func=AF.Sigmoid),
                         r=[B_PB[bb]], w=[B_TMPF[ti]])
                    P.op("dve", partial(V_.tensor_tensor, out=gluv(jc, 30, segn), in0=segview(PB[ba][:, 0:T]),
                                        in1=segview(TMPF[:, ti, 0:T]), op=ALU.mult),
                         r=[B_PB[ba], B_TMPF[ti]], w=[B_GLU[jc]])
                w_done(2)
            def conv_ops():
                for jc in range(8):
                    accv = segview(ACC[:, jc, 0:T])
                    cwb = PP_CW + jc * 31
                    yield partial(P.op, "dve", partial(V_.tensor_scalar, out=accv, in0=gluv(jc, 0, segn),
                                                       scalar1=pp(l, cwb), scalar2=pp(l, PP_CB + jc),
                                                       op0=ALU.mult, op1=ALU.add),
                                  r=[B_GLU[jc], B_PP], w=[B_ACC[jc]])
                    for k in range(1, CW):
                        yield partial(P.op, "dve", partial(V_.scalar_tensor_tensor, out=accv, in0=gluv(jc, k, segn),
                                                           scalar=pp(l, cwb + k), in1=accv, op0=ALU.mult, op1=ALU.add),
                                      r=[B_GLU[jc], B_ACC[jc], B_PP], w=[B_ACC[jc]], nosync=(segn >= 512))
                if kind == "p":
                    yield partial(P.op, "pool", partial(G_.tensor_copy, out=HIST[:, l], in_=GLU[:, :, 512:542]),
                                  r=B_GLU, w=[B_HIST[l]])
            conv_it = conv_ops()
            n_conv_ops = 8 * CW + 1

            conv_state = {"done": 0}

            def emit_conv(k):
                for _ in range(k):
                    f = next(conv_it, None)
                    if f is None:
                        return
                    f()
                    conv_state["done"] += 1

            gq = [w_get(l, G_Q0 + i) for i in range(4)]

            def qk_mm(s):
                QKV, B_QKV = QKVs[s % 2], B_QKVs[s % 2]
                for i in range(4):
                    bk = next_bank()
                    for kc in range(8):
                        P.op("pe", partial(T_.matmul, PB[bk][:, :], lhsT=XT[:, kc, s * 128:(s + 1) * 128], rhs=gq[i][0][:, kc, :],
                                           start=(kc == 0), stop=(kc == 7)), r=[gq[i][1], B_XT[s]], w=[B_PB[bk]])
                    P.op("act", partial(A_.copy, out=QKV[:, i * 512:(i + 1) * 512], in_=PB[bk][:, :]), r=[B_PB[bk]], w=[B_QKV])

            def qk_post(s):
                QKV, B_QKV = QKVs[s % 2], B_QKVs[s % 2]
                if kind == "p":
                    P.dma("sp", CS[:, 0, :], cosx[row0 + s * 128: row0 + (s + 1) * 128, :], w=[B_CS])
                    P.dma("sp", CS[:, 1, :], sinx[row0 + s * 128: row0 + (s + 1) * 128, :], w=[B_CS])
                else:
                    for b in range(2):
                        P.dma("sp", CS[b * 64:(b + 1) * 64, 0, :], cosx[SEQ:SEQ + 64, :], w=[B_CS])
                        P.dma("sp", CS[b * 64:(b + 1) * 64, 1, :], sinx[SEQ:SEQ + 64, :], w=[B_CS])
                qv = QKV[:].rearrange("p (g d) -> p g d", d=64)
                t1 = qv[:, :, 0:8]
                t2 = qv[:, :, 8:16]
                cosv = CS[:, 0, :].rearrange("p (g i) -> p g i", i=8)
                sinv = CS[:, 1, :].rearrange("p (g i) -> p g i", i=8)
                rt = [RT[:, i, :].rearrange("p (g i) -> p g i", i=8) for i in range(4)]
                rr_ = [B_QKV, B_CS]
                P.op("pool", partial(G_.tensor_tensor, out=rt[0], in0=t1, in1=cosv, op=ALU.mult), r=rr_, w=[B_RT])
                P.op("pool", partial(G_.tensor_tensor, out=rt[1], in0=t2, in1=sinv, op=ALU.mult), r=rr_, w=[B_RT])
                P.op("pool", partial(G_.tensor_tensor, out=rt[2], in0=t1, in1=sinv, op=ALU.mult), r=rr_, w=[B_RT])
                P.op("pool", partial(G_.tensor_tensor, out=rt[3], in0=t2, in1=cosv, op=ALU.mult), r=rr_, w=[B_RT])
                P.op("pool", partial(G_.tensor_tensor, out=t1, in0=rt[0], in1=rt[1], op=ALU.subtract), r=[B_RT], w=[B_QKV])
                P.op("pool", partial(G_.tensor_tensor, out=t2, in0=rt[3], in1=rt[2], op=ALU.add), r=[B_RT], w=[B_QKV])
                P.dma("pool", nk_dst[l, row0 + s * 128: row0 + (s + 1) * 128, :], QKV[:, 1024:2048], r=[B_QKV])
                for part, (dst, bdst) in enumerate(((None, B_QT), (KTC, B_KTC))):
                    XB, B_XB = next_xb()
                    P.op("dve", partial(V_.tensor_copy, out=XB[:], in_=QKV[:, part * 1024:(part + 1) * 1024]), r=[B_QKV], w=[B_XB])
                    bk = next_bank()
                    pv = PB[bk][:].bitcast(BF16).rearrange("p (a b) -> p a b", a=8)
                    for hh in range(8):
                        c0 = hh * 128
                        P.op("pe", partial(T_.transpose, out=pv[:, hh, :], in_=XB[:, c0:c0 + 128], identity=IDB[:]),
                             r=[B_XB, B_ID], w=[B_PB[bk]])
                    if part == 0:
                        P.op("act", partial(A_.copy, out=QTZ[0:64, 0, :, s * 128:(s + 1) * 128], in_=pv[0:64]), r=[B_PB[bk]], w=[bdst[s]])
                        P.op("dve", partial(V_.tensor_copy, out=QTZ[64:128, 1, :, s * 128:(s + 1) * 128], in_=pv[64:128]),
                             r=[B_PB[bk]], w=[bdst[s]])
                    else:
                        P.op("act", partial(A_.copy, out=dst[:, :, s * 128:(s + 1) * 128], in_=pv), r=[B_PB[bk]], w=[bdst[s]])

            qk_mm(0)
            for s in SS:
                if s + 1 < nsub:
                    qk_mm(s + 1)
                qk_post(s)
                emit_conv(20 if kind == "p" else 40)
            w_done(4)
            gv = [w_get(l, G_V0 + i) for i in range(2)]
            for s in SS:
                QKV, B_QKV = QKVs[s % 2], B_QKVs[s % 2]
                for i in range(2):
                    bk = next_bank()
                    for kc in range(8):
                        P.op("pe", partial(T_.matmul, PB[bk][:, :], lhsT=XT[:, kc, s * 128:(s + 1) * 128], rhs=gv[i][0][:, kc, :],
                                           start=(kc == 0), stop=(kc == 7)), r=[gv[i][1], B_XT[s]], w=[B_PB[bk]])
                    P.op("act", partial(A_.copy, out=QKV[:, i * 512:(i + 1) * 512], in_=PB[bk][:, :]), r=[B_PB[bk]], w=[B_QKV])
                P.dma("pool", nv_dst[l, row0 + s * 128: row0 + (s + 1) * 128, :], QKV[:, 0:1024], r=[B_QKV])
                P.op("dve", partial(V_.tensor_copy, out=VC[:, s, :, 0:128], in_=QKV[:, 0:1024].rearrange("p (h e) -> p h e", h=8)),
                     r=[B_QKV], w=[B_VC[s]])
                emit_conv(12 if kind == "p" else 24)
            w_done(2)
            if kind == "p" and t < N_PROMPT_TILES - 1:
                P.dma("pool", kTs[l, :, :, row0:row0 + 512].rearrange("h p k -> p h k"), KTC[:, :, :],
                      r=B_KTC, w=[B_kTs[l][t]])
                for s in SS:
                    P.dma("pool", vsc[l, row0 + s * 128:row0 + (s + 1) * 128].rearrange("p h e -> p h e"), VC[:, s, :, 0:128],
                          r=[B_VC[s]], w=[B_vsc[l][t * 4 + s]])

            def bank_of(i):
                return 2 * ((i // 2) % 2) + (i % 2)

            def emit_qk(i):
                u = units[i]
                if u["load"] is not None:
                    ensure_load(u["load"])
                bk = bank_of(i)
                c = u["c"]
                n = u["q1"] - u["q0"]
                P.op("pe", partial(T_.matmul, PB[bk][u["kp0"]:u["kp0"] + u["nk"], 0:n], lhsT=u["lhsT"],
                                   rhs=QTZ[:, c, u["h"], u["q0"]:u["q1"]], start=True, stop=True),
                     r=u["rk"] + B_QT[:nsub], w=[B_PB[bk]])

            def emit_exp_pair(p):
                u = units[2 * p]
                b0 = 2 * (p % 2)
                n = u["q1"] - u["q0"]
                kp0, nk = u["kp0"], u["nk"]
                pc0 = u.get("ptc0", 0)
                if u.get("zero_other"):
                    oc = (1 - u["b"]) * 64
                    for k_ in range(4):
                        P.op("pool", partial(G_.memset, PT[k_][:, oc:oc + 64], 0.0), w=[B_PT[k_]])
                src = PSALL[kp0:kp0 + nk, b0 * 512:(b0 + 2) * 512].rearrange("p (c n) -> p c n", c=2)[:, :, 0:n]
                dst = PTMEM[kp0:kp0 + nk, b0 * 512:(b0 + 2) * 512].rearrange("p (c n) -> p c n", c=2)[:, :, pc0:pc0 + n]
                P.op("act", partial(A_.activation, out=dst, in_=src, func=AF.Exp, scale=0.125),
                     r=[B_PB[b0], B_PB[b0 + 1]], w=[B_PT[b0], B_PT[b0 + 1]])
                if u["mask"] is not None:
                    p0, p1, c0, c1 = u["mask"]
                    mv = PTMEM[p0:p1, b0 * 512:(b0 + 2) * 512].rearrange("p (c n) -> p c n", c=2)[:, :, c0 - u["q0"]:c1 - u["q0"]]
                    P.op("pool", partial(G_.memset, mv, 0.0), w=[B_PT[b0], B_PT[b0 + 1]])

            def emit_pv(i):
                u = units[i]
                bk = bank_of(i)
                kp0, nk = u["kp0"], u["nk"]
                c = u["c"]
                qoff = u["q0"] if "ptc0" not in u else 0
                for (s, qa, qn, op0), st in zip(u["pv"], u["start"]):
                    accb = PB[4 + s][:, 0:260].rearrange("p (c e) -> p c e", c=2)
                    P.op("pe", partial(T_.matmul, accb[op0:op0 + qn, c, :], lhsT=PT[bk][kp0:kp0 + nk, qa - qoff:qa - qoff + qn],
                                       rhs=u["vrhs"], start=st, stop=True, skip_group_check=True),
                         r=[B_PT[bk]] + u["rv"], w=[B_PB[4 + s]])

            def finalize_head(hh):
                for s in SS:
                    accb = PB[4 + s][:, 0:260].rearrange("p (c e) -> p c e", c=2)
                    o = 8 * s
                    P.op("dve", partial(V_.reciprocal, out=SM[:, o:o + 2], in_=accb[:, :, 128]), r=[B_PB[4 + s]], w=[B_SM])
                    P.op("dve", partial(V_.tensor_scalar, out=SM[:, o + 2:o + 3], in0=SM[:, o + 1:o + 2], scalar1=LAMT[:, l, 1:2],
                                        scalar2=None, op0=ALU.mult), r=[B_SM, B_LAM], w=[B_SM])
                    P.op("dve", partial(V_.tensor_scalar, out=TMPO[:, s % 2, :], in0=accb[:, 1, 0:128], scalar1=SM[:, o + 2:o + 3],
                                        scalar2=None, op0=ALU.mult), r=[B_PB[4 + s], B_SM], w=[B_TMPO[s % 2]])
                    P.op("dve", partial(V_.scalar_tensor_tensor, out=OF[:, s, :], in0=accb[:, 0, 0:128], scalar=SM[:, o:o + 1],
                                        in1=TMPO[:, s % 2, :], op0=ALU.mult, op1=ALU.add),
                         r=[B_PB[4 + s], B_SM, B_TMPO[s % 2]], w=[B_OF])
                for s in SS:
                    P.op("dve", partial(V_.tensor_tensor, out=TMPO[:, s % 2, :], in0=OF[:, s, :], in1=OF[:, s, :], op=ALU.mult),
                         r=[B_OF], w=[B_TMPO[s % 2]])
                    P.op("dve", partial(V_.reduce_sum, out=SM[:, 32 + s:33 + s], in_=TMPO[:, s % 2, :], axis=mybir.AxisListType.X),
                         r=[B_TMPO[s % 2]], w=[B_SM])
                P.op("pool", partial(G_.tensor_scalar, out=SM[:, 40:40 + nsub], in0=SM[:, 32:32 + nsub], scalar1=1.0 / 128, scalar2=EPS,
                                     op0=ALU.mult, op1=ALU.add), r=[B_SM], w=[B_SM])
                P.op("pool", partial(G_.tensor_tensor, out=SM[:, 40:40 + nsub], in0=SM[:, 40:40 + nsub], in1=MH[:, 0:nsub], op=ALU.pow),
                     r=[B_SM, B_CONST], w=[B_SM])
                for s in SS:
                    P.op("dve", partial(V_.scalar_tensor_tensor, out=OB[:, s, hh * 128:(hh + 1) * 128], in0=OF[:, s, :],
                                        scalar=SM[:, 40 + s:41 + s], in1=GN[:, l, :], op0=ALU.mult, op1=ALU.mult),
                         r=[B_OF, B_SM, B_GN], w=[B_OB[s]])

            npairs = nun // 2
            assert nun % 2 == 0
            for i_ in range(min(4, nun)):
                emit_qk(i_)
            for p in range(npairs):
                u = units[2 * p]
                if u["load"] is not None and u["load"] + 1 < len(loads):
                    ensure_load(u["load"] + 1)
                emit_exp_pair(p)
                emit_pv(2 * p)
                emit_pv(2 * p + 1)
                if p + 2 < npairs:
                    emit_qk(2 * p + 4)
                    emit_qk(2 * p + 5)
                if p == 0:
                    conv_left = n_conv_ops - conv_state["done"]
                    conv_base = conv_state["done"]
                tgt = conv_base + -(-(2 * p + 2) * conv_left // nun)
                emit_conv(max(0, tgt - conv_state["done"]))
                if units[2 * p + 1].get("last_of_head"):
                    finalize_head(u["h"])
            emit_conv(n_conv_ops)

            dump(f"acc_{kind}{t}_{l}", ACC[:, :, 0:T], B_ACC)
            dump(f"ob_{kind}{t}_{l}", OB[:, 0:nsub, :], B_OB[:nsub])
            dump(f"lam_{kind}{t}_{l}", LAMT[:, l, :], [B_LAM])
            def emit_new_conv(src_of_jc, dst_ap):
                bk0 = next_bank()
                bk1 = next_bank()
                for jc in range(8):
                    bk = bk0 if jc < 4 else bk1
                    P.op("pe", partial(T_.transpose, out=PB[bk][0:30, (jc % 4) * 128:(jc % 4 + 1) * 128], in_=src_of_jc(jc),
                                       identity=IDF[:]), r=B_GLU + [B_HIST[l], B_ID], w=[B_PB[bk]])
                for hf, bkx in enumerate((bk0, bk1)):
                    ti = next_tmpf()
                    P.op("act", partial(A_.copy, out=TMPF[0:30, ti, :], in_=PB[bkx][0:30, :]), r=[B_PB[bkx]], w=[B_TMPF[ti]])
                    P.dma("pool", dst_ap[:, hf * 512:(hf + 1) * 512], TMPF[0:30, ti, :], r=[B_TMPF[ti]])
            if kind == "p" and t == n_prompt_tiles - 1:
                emit_new_conv(lambda jc: HIST[:, l, jc, :], ncv_p[l])
            if kind == "s":
                for b in range(2):
                    emit_new_conv(lambda jc, b=b: GLU[:, jc, b * segw + segn: b * segw + segw], ncv_s[l, b])

            bs1 = next_bank()
            bs2 = next_bank()
            for jc in range(8):
                ti = next_tmpf()
                P.op("pool", partial(G_.tensor_tensor, out=TMPF[:, ti, 0:T], in0=ACC[:, jc, 0:T], in1=ACC[:, jc, 0:T], op=ALU.mult),
                     r=[B_ACC[jc]], w=[B_TMPF[ti]])
                P.op("pe", partial(T_.matmul, PB[bs1][0:1, 0:T], lhsT=ONES[:, 0:1], rhs=ACC[:, jc, 0:T], start=(jc == 0), stop=(jc == 7)),
                     r=[B_ACC[jc], B_CONST], w=[B_PB[bs1]])
                P.op("pe", partial(T_.matmul, PB[bs2][0:1, 0:T], lhsT=ONES[:, 0:1], rhs=TMPF[:, ti, 0:T], start=(jc == 0), stop=(jc == 7)),
                     r=[B_TMPF[ti], B_CONST], w=[B_PB[bs2]])
            R = lambda i: ROWS[0:1, i, 0:T]
            P.op("dve", partial(V_.tensor_scalar, out=R(0), in0=PB[bs1][0:1, 0:T], scalar1=1.0 / D, scalar2=None, op0=ALU.mult),
                 r=[B_PB[bs1]], w=[B_ROWS])
            P.op("dve", partial(V_.tensor_tensor, out=R(1), in0=R(0), in1=R(0), op=ALU.mult), r=[B_ROWS], w=[B_ROWS])
            P.op("dve", partial(V_.scalar_tensor_tensor, out=R(1), in0=PB[bs2][0:1, 0:T], scalar=1.0 / D, in1=R(1),
                                op0=ALU.mult, op1=ALU.subtract), r=[B_PB[bs2], B_ROWS], w=[B_ROWS])
            P.op("act", partial(A_.activation, out=R(2), in_=R(1), func=AF.Sqrt, bias=EPSC[0:1, :]), r=[B_ROWS, B_CONST], w=[B_ROWS])
            P.op("dve", partial(V_.reciprocal, out=R(2), in_=R(2)), r=[B_ROWS], w=[B_ROWS])
            P.op("dve", partial(V_.scalar_tensor_tensor, out=R(0), in0=R(0), scalar=-1.0, in1=R(2), op0=ALU.mult, op1=ALU.mult),
                 r=[B_ROWS], w=[B_ROWS])
            for s in SS:
                bk = next_bank()
                pv = PB[bk][:].bitcast(BF16).rearrange("p (a b) -> p a b", a=8)
                for hh in range(8):
                    P.op("pe", partial(T_.transpose, out=pv[:, hh, :], in_=OB[:, s, hh * 128:(hh + 1) * 128], identity=IDB[:]),
                         r=[B_OB[s], B_ID], w=[B_PB[bk]])
                P.op("act", partial(A_.copy, out=OT[:, :, s * 128:(s + 1) * 128], in_=pv), r=[B_PB[bk]], w=[B_OT[s]])

            P.op("pe", partial(T_.matmul, PB[bs1][:, 0:T], lhsT=ONES[0:1, :], rhs=R(2), start=True, stop=True),
                 r=[B_ROWS, B_CONST], w=[B_PB[bs1]])
            P.op("pe", partial(T_.matmul, PB[bs2][:, 0:T], lhsT=ONES[0:1, :], rhs=R(0), start=True, stop=True),
                 r=[B_ROWS, B_CONST], w=[B_PB[bs2]])
            for jc in range(8):
                P.op("dve", partial(V_.tensor_tensor, out=ACC[:, jc, 0:T], in0=ACC[:, jc, 0:T], in1=PB[bs1][:, 0:T], op=ALU.mult),
                     r=[B_ACC[jc], B_PB[bs1]], w=[B_ACC[jc]])
                P.op("dve", partial(V_.tensor_tensor, out=ACC[:, jc, 0:T], in0=ACC[:, jc, 0:T], in1=PB[bs2][:, 0:T], op=ALU.add),
                     r=[B_ACC[jc], B_PB[bs2]], w=[B_ACC[jc]])
                P.op("act", partial(A_.activation, out=HCT[:, jc, 0:T], in_=ACC[:, jc, 0:T], func=AF.Silu,
                                    scale=pp(l, PP_LG + jc), bias=pp(l, PP_LB + jc)), r=[B_ACC[jc], B_PP], w=[B_HCT[jc]])

            dump(f"hct_{kind}{t}_{l}", HCT[:, :, 0:T], B_HCT)
            dump(f"ot_{kind}{t}_{l}", OT[:, :, 0:T], B_OT[:nsub])
            if last_layer and nxt is not None:
                nkind, nt = nxt
                nsrc = xp if nkind == "p" else xs
                nrow0 = nt * 512 if nkind == "p" else 0
                for s2 in range(4 if nkind == "p" else 1):
                    qv_, qb_ = qkv_view(s2)
                    P.dma("sp", qv_, nsrc[nrow0 + s2 * 128: nrow0 + (s2 + 1) * 128, :], w=qb_[:1])
            for half in range(2):
                ggc = w_get(l, G_GC0 + 4 * half)
                gga = w_get(l, G_GA0 + 4 * half)
                gco = w_get(l, G_CO0 + 4 * half)
                gao = w_get(l, G_AO0 + 4 * half)
                for j in range(4):
                    jc = half * 4 + j
                    banks = [next_bank() for _ in range(4)]
                    srcs = ((ggc, XT, B_XT[:nsub]), (gga, XT, B_XT[:nsub]), (gco, HCT, B_HCT), (gao, OT, B_OT[:nsub]))
                    for (gw, src, bsrc), bk in zip(srcs, banks):
                        for kc in range(8):
                            P.op("pe", partial(T_.matmul, PB[bk][:, 0:T], lhsT=gw[0][:, kc, j * 128:(j + 1) * 128], rhs=src[:, kc, 0:T],
                                               start=(kc == 0), stop=(kc == 7)), r=[gw[1]] + list(bsrc), w=[B_PB[bk]])
                    t0 = next_tmpf()
                    t1_ = next_tmpf()
                    P.op("act", partial(A_.activation, out=TMPF[:, t0, 0:T], in_=PB[banks[0]][:, 0:T], func=AF.Sigmoid,
                                        bias=pp(l, PP_BG + jc)), r=[B_PB[banks[0]], B_PP], w=[B_TMPF[t0]])
                    P.op("act", partial(A_.activation, out=TMPF[:, t1_, 0:T], in_=PB[banks[1]][:, 0:T], func=AF.Sigmoid,
                                        bias=pp(l, PP_BG + 8 + jc)), r=[B_PB[banks[1]], B_PP], w=[B_TMPF[t1_]])
                    P.op("dve", partial(V_.tensor_tensor, out=TMPF[:, t0, 0:T], in0=PB[banks[2]][:, 0:T], in1=TMPF[:, t0, 0:T], op=ALU.mult),
                         r=[B_PB[banks[2]], B_TMPF[t0]], w=[B_TMPF[t0]])
                    P.op("dve", partial(V_.tensor_tensor, out=TMPF[:, t1_, 0:T], in0=PB[banks[3]][:, 0:T], in1=TMPF[:, t1_, 0:T], op=ALU.mult),
                         r=[B_PB[banks[3]], B_TMPF[t1_]], w=[B_TMPF[t1_]])
                    P.op("pool", partial(G_.tensor_tensor, out=MT[:, jc, 0:T], in0=TMPF[:, t0, 0:T], in1=TMPF[:, t1_, 0:T], op=ALU.add),
                         r=[B_TMPF[t0], B_TMPF[t1_]], w=B_MT[:nsub])
                w_done(4)
            dump(f"mt_{kind}{t}_{l}", MT[:, :, 0:T], B_MT[:nsub])
            gwo = [w_get(l, G_WO0 + i) for i in range(2)]
            for s in SS:
                for i in range(2):
                    bk = next_bank()
                    for kc in range(8):
                        P.op("pe", partial(T_.matmul, PB[bk][:, :], lhsT=MT[:, kc, s * 128:(s + 1) * 128], rhs=gwo[i][0][:, kc, :],
                                           start=(kc == 0), stop=(kc == 7)), r=[gwo[i][1]] + B_MT[:nsub], w=[B_PB[bk]])
                    P.op("dve", partial(V_.scalar_tensor_tensor, out=XR[:, s, i * 512:(i + 1) * 512], in0=XR[:, s, i * 512:(i + 1) * 512],
                                        scalar=ALPHA, in1=PB[bk][:, :], op0=ALU.mult, op1=ALU.add),
                         r=[*B_XR[s], B_PB[bk]], w=[*B_XR[s]])
                layer_norm_one(s, 0, 1)
            w_done(2)
            for s in SS:
                make_xT(s)
            if last_layer and nxt is not None:
                P.dma("sp", GB[:, 0, :], rv_d[RV_LNIN_G:RV_LNIN_G + D].partition_broadcast(128), w=[B_GB[0]])
                P.dma("sp", GB[:, 1, :], rv_d[RV_LNIN_B:RV_LNIN_B + D].partition_broadcast(128), w=[B_GB[1]])
                early_ln = [s2 for s2 in range(4 if nxt[0] == "p" else 1)]
            else:
                early_ln = []
            dump(f"x1_{kind}{t}_{l}", XR[:, 0:nsub, :], [b_ for p_ in B_XR[:nsub] for b_ in p_])
            for gi in range(8):
                gu = w_get(l, G_UP + gi)
                for j in range(4):
                    f = gi * 4 + j
                    bk = next_bank()
                    for kc in range(8):
                        P.op("pe", partial(T_.matmul, PB[bk][:, 0:T], lhsT=gu[0][:, kc, j * 128:(j + 1) * 128], rhs=XT[:, kc, 0:T],
                                           start=(kc == 0), stop=(kc == 7)), r=[gu[1]] + B_XT[:nsub], w=[B_PB[bk]])
                    ti = next_tmpf()
                    P.op("act", partial(A_.activation, out=TMPF[:, ti, 0:T], in_=PB[bk][:, 0:T], func=AF.Relu, bias=pp(l, PP_BU + f)),
                         r=[B_PB[bk], B_PP], w=[B_TMPF[ti]])
                    P.op("pool", partial(G_.tensor_tensor, out=HT[:, f, 0:T], in0=TMPF[:, ti, 0:T], in1=TMPF[:, ti, 0:T], op=ALU.mult),
                         r=[B_TMPF[ti]], w=[B_HT[f]])
                w_done(1)
                if gi % 2 == 1 and early_ln:
                    s2 = early_ln.pop(0)
                    qv_, qb_ = qkv_view(s2)
                    layer_norm_one(s2, 0, 1, xr=qv_, xb=qb_)
            for i in range(2):
                banks = [4 * i + s for s in SS]
                gds = [w_get(l, G_DN + i * 4 + kq) for kq in range(4)]
                for s in SS:
                    for kq in range(4):
                        gd = gds[kq]
                        for kc in range(8):
                            f = kq * 8 + kc
                            P.op("pe", partial(T_.matmul, PB[banks[s]][:, :], lhsT=HT[:, f, s * 128:(s + 1) * 128], rhs=gd[0][:, kc, :],
                                               start=(f == 0), stop=(f == 31)), r=[gd[1], B_HT[f]], w=[B_PB[banks[s]]])
                    xs_ = XR[:, s, i * 512:(i + 1) * 512]
                    P.op("dve", partial(V_.scalar_tensor_tensor, out=xs_, in0=xs_, scalar=ALPHA, in1=PB[banks[s]][:, :],
                                        op0=ALU.mult, op1=ALU.add), r=[*B_XR[s], B_PB[banks[s]]], w=[*B_XR[s]])
                    if i == 1:
                        P.op("pool", partial(G_.tensor_tensor, out=XR[:, s, :], in0=XR[:, s, :], in1=GB[:, 4, :], op=ALU.add),
                             r=[*B_XR[s], B_GB[4]], w=[*B_XR[s]])
                        layer_norm_one(s, 2, 3)
                        if s == nsub - 1:
                            dump(f"x2_{kind}{t}_{l}", XR[:, 0:nsub, :], [b_ for p_ in B_XR[:nsub] for b_ in p_])
                        if last_layer:
                            y_dst = y_p if kind == "p" else y_s
                            P.dma("pool", y_dst[row0 + s * 128: row0 + (s + 1) * 128, :], XR[:, s, :], r=[*B_XR[s]])
                w_done(4)
            rr["bank"] = 0

    order = [("p", t) for t in range(n_prompt_tiles)] + ([("s", 0)] if do_sample else [])
    for i, (k_, t_) in enumerate(order):
        run_tile(k_, t_, prefetched=(i > 0), nxt=(order[i + 1] if i + 1 < len(order) else None))
    P.emit()
    return nc


def _offset_of(handle):
    for attr in ("offset", "addr", "address", "start"):
        if hasattr(handle, attr):
            v = getattr(handle, attr)
            v = v() if callable(v) else v
            if isinstance(v, int):
                return v
    raise RuntimeError("cannot find sbuf offset: " + str([a for a in dir(handle) if not a.startswith("_")]))


def _granules(w_in, w_conv_out, w_attn_out, w_out, w_up, w_down):
    out = np.empty((DEPTH, NGRAN, 128, 8, 512), np.float32)

    def g1024(W):
        n = W.shape[1] // 512
        return W.reshape(8, 128, n, 512).transpose(2, 1, 0, 3)
    for l in range(DEPTH):
        gi = g1024(w_in[l])
        out[l, G_A0], out[l, G_A1], out[l, G_B0], out[l, G_B1] = gi[0], gi[1], gi[2], gi[3]
        out[l, G_Q0:G_Q0 + 6] = gi[4:10]
        out[l, G_GC0], out[l, G_GC1], out[l, G_GA0], out[l, G_GA1] = gi[10], gi[11], gi[12], gi[13]
        co = g1024(w_conv_out[l])
        ao = g1024(w_attn_out[l])
        out[l, G_CO0], out[l, G_CO1] = co[0], co[1]
        out[l, G_AO0], out[l, G_AO1] = ao[0], ao[1]
        out[l, G_WO0:G_WO0 + 2] = g1024(w_out[l])
        out[l, G_UP:G_UP + 8] = g1024(w_up[l])
        dn = w_down[l].reshape(4, 8, 128, 2, 512).transpose(3, 0, 2, 1, 4)
        out[l, G_DN:G_DN + 8] = dn.reshape(8, 128, 8, 512)
    return out.reshape(DEPTH, NGRAN, 128, 4096)


def _host_prep(inp):
    f = lambda k: np.asarray(inp[k], dtype=np.float32)
    wgr = _granules(f("w_in"), f("w_conv_out"), f("w_attn_out"), f("w_out"), f("w_up"), f("w_down"))
    pp = np.empty((128, DEPTH * PP_L), np.float32)
    for l in range(DEPTH):
        o = l * PP_L
        pp[:, o + PP_BG:o + PP_BG + 16] = f("b_gate")[l].reshape(16, 128).T
        pp[:, o + PP_CW:o + PP_CW + 248] = f("conv_dw_w")[l].reshape(CW, 8, 128).transpose(2, 1, 0).reshape(128, 248)
        pp[:, o + PP_CB:o + PP_CB + 8] = f("conv_dw_b")[l].reshape(8, 128).T
        pp[:, o + PP_LG:o + PP_LG + 8] = f("conv_ln_g")[l].reshape(8, 128).T
        pp[:, o + PP_LB:o + PP_LB + 8] = f("conv_ln_b")[l].reshape(8, 128).T
        pp[:, o + PP_BU:o + PP_BU + 32] = f("b_up")[l].reshape(32, 128).T
    rv = np.empty((RV_L0 + DEPTH * RV_LSZ,), np.float32)
    rv[RV_LNIN_G:RV_LNIN_G + D] = f("ln_in_g")
    rv[RV_LNIN_B:RV_LNIN_B + D] = f("ln_in_b")
    for l in range(DEPTH):
        b = RV_L0 + l * RV_LSZ
        for off, k, n in ((RV_LN1G, "ln1_g", D), (RV_LN1B, "ln1_b", D), (RV_LN2G, "ln2_g", D), (RV_LN2B, "ln2_b", D),
                          (RV_BD, "b_down", D), (RV_ANG, "attn_norm_g", 128), (RV_LQ1, "lambda_q1", 64),
                          (RV_LK1, "lambda_k1", 64), (RV_LQ2, "lambda_q2", 64), (RV_LK2, "lambda_k2", 64)):
            rv[b + off:b + off + n] = f(k)[l]
    pos = np.concatenate([np.arange(SEQ), PAST + np.arange(DSEQ)]).astype(np.float32)
    inv = (np.float32(THETA) ** (-np.arange(8, dtype=np.float32) * np.float32(2.0) / np.float32(16))).astype(np.float32)
    ang = (pos[:, None] * inv[None, :]).astype(np.float32)
    cosx = np.tile(np.cos(ang).astype(np.float32), (1, 32))
    sinx = np.tile(np.sin(ang).astype(np.float32), (1, 32))
    ident = np.eye(128, dtype=np.float32)
    shared = dict(wg=wgr, pp=pp, rv=rv, cosx=cosx, sinx=sinx, ident=ident)
    xp_all = f("x_prompt")
    xs_all = f("x_sample")
    ck = f("cache_k")
    cvv = f("cache_v")
    sc = f("state_conv")
    in_maps = []
    for c in range(NCORES):
        m = dict(shared)
        m["xp"] = np.ascontiguousarray(xp_all[c])
        m["xs"] = np.ascontiguousarray(xs_all[2 * c:2 * c + 2].reshape(128, D))
        m["ckT"] = np.ascontiguousarray(ck[:, 2 * c:2 * c + 2].transpose(0, 1, 4, 3, 2))
        m["cv"] = np.ascontiguousarray(cvv[:, 2 * c:2 * c + 2])
        m["scT"] = np.ascontiguousarray(sc[:, 2 * c:2 * c + 2].reshape(DEPTH, 2, 30, 8, 128).transpose(0, 1, 4, 3, 2))
        in_maps.append(m)
    return in_maps


def _assemble(results):
    y_p = np.stack([r["y_p"] for r in results])
    y_s = np.concatenate([r["y_s"].reshape(2, DSEQ, D) for r in results], 0)
    nk_p = np.stack([r["nk_p"] for r in results], 1).reshape(DEPTH, NCORES, SEQ, H, 128)
    nv_p = np.stack([r["nv_p"] for r in results], 1).reshape(DEPTH, NCORES, SEQ, H, 128)
    ncv_p = np.stack([r["ncv_p"] for r in results], 1)
    nk_s = np.concatenate([r["nk_s"].reshape(DEPTH, 2, DSEQ, H, 128) for r in results], 1)
    nv_s = np.concatenate([r["nv_s"].reshape(DEPTH, 2, DSEQ, H, 128) for r in results], 1)
    ncv_s = np.concatenate([r["ncv_s"] for r in results], 1)
    return tuple(np.ascontiguousarray(a, dtype=np.float32) for a in (y_p, y_s, nk_p, nv_p, ncv_p, nk_s, nv_s, ncv_s))


def kernel(**inputs):
    in_maps = _host_prep(inputs)
    nc = build_program()
    res = run_bass_kernel_spmd(nc, in_maps, core_ids=list(range(NCORES)))
    return _assemble(res.results)
```

```python
import math
from functools import partial
import numpy as np
import concourse.bass as bass
import concourse.mybir as mybir
from concourse.bass_utils import run_bass_kernel_spmd

F32 = mybir.dt.float32
BF16 = mybir.dt.bfloat16
AF = mybir.ActivationFunctionType
ALU = mybir.AluOpType

D = 1024
SEQ = 8192
DEPTH = 2
NCORES = 8
PAST = 2048
DSEQ = 64
H = 8
CW = 31
EPS = 1e-5
ALPHA = (2 * DEPTH) ** 0.25
THETA = 500000.0
NGRAN = 36
KCHUNK = 1024
NWBUF = 5
G_A0, G_B0, G_A1, G_B1, G_Q0, G_Q1, G_K0, G_K1, G_V0, G_V1 = range(10)
G_GC0, G_GA0, G_CO0, G_AO0, G_GC1, G_GA1, G_CO1, G_AO1, G_WO0, G_WO1 = range(10, 20)
G_UP = 20
G_DN = 28
PP_BG, PP_CW, PP_CB, PP_LG, PP_LB, PP_BU = 0, 16, 16 + 248, 16 + 256, 16 + 264, 16 + 272
PP_L = 16 + 272 + 32
RV_LNIN_G, RV_LNIN_B = 0, 1024
RV_L0 = 2048
RV_LN1G, RV_LN1B, RV_LN2G, RV_LN2B, RV_BD, RV_ANG, RV_LQ1, RV_LK1, RV_LQ2, RV_LK2 = (
    0, 1024, 2048, 3072, 4096, 5120, 5248, 5312, 5376, 5440)
RV_LSZ = 5504
N_PROMPT_TILES = SEQ // 512


class Buf:
    __slots__ = ("name", "lw", "rd", "alias")

    def __init__(self, name):
        self.name = name
        self.lw = None
        self.rd = []
        self.alias = ()


class Sem:
    __slots__ = ("h", "count")

    def __init__(self, h):
        self.h = h
        self.count = 0


class Op:
    __slots__ = ("eng", "fn", "deps", "sig", "sem", "sigval", "dma")

    def __init__(self, eng, fn, dma):
        self.eng = eng
        self.fn = fn
        self.deps = ()
        self.sig = False
        self.sem = None
        self.sigval = 0
        self.dma = dma


class Prog:
    def __init__(self, nc):
        self.nc = nc
        self.ops = []
        self.eng_obj = {"pe": nc.tensor, "act": nc.scalar, "dve": nc.vector, "pool": nc.gpsimd, "sp": nc.sync}
        self.eng_sem = {k: Sem(nc.alloc_semaphore("s_" + k)) for k in self.eng_obj}
        self.dma_sems = {"sp": [Sem(nc.alloc_semaphore(f"d_sp{i}")) for i in range(16)],
                         "pool": [Sem(nc.alloc_semaphore(f"d_po{i}")) for i in range(12)],
                         "act": [Sem(nc.alloc_semaphore(f"d_ac{i}")) for i in range(6)]}
        self.dma_rr = {"sp": 0, "pool": 0, "act": 0}

    def _record(self, o, r, w, nosync):
        deps = set()
        eng = o.eng
        for b in r:
            lw = b.lw
            if lw is not None:
                if nosync and (not lw.dma) and lw.eng == eng and not o.dma:
                    continue
                deps.add(lw)
        wexp = list(w)
        for b in w:
            if b.alias:
                wexp.extend(b.alias)
        for b in wexp:
            lw = b.lw
            if lw is not None and (o.dma or lw.dma or lw.eng != eng):
                deps.add(lw)
            for rd in b.rd:
                if o.dma or rd.dma or rd.eng != eng:
                    deps.add(rd)
        deps.discard(o)
        for d in deps:
            d.sig = True
        o.deps = tuple(deps)
        for b in wexp:
            b.lw = o
            b.rd = []
        for b in r:
            if b.lw is o:
                continue
            if not o.dma:
                b.rd = [x for x in b.rd if x.dma or x.eng != eng]
            b.rd.append(o)
        self.ops.append(o)
        return o

    def op(self, eng, fn, r=(), w=(), nosync=False):
        return self._record(Op(eng, fn, False), r, w, nosync)

    def dma(self, q, out, in_, r=(), w=(), **kw):
        o = Op(q, partial(self.eng_obj[q].dma_start, out=out, in_=in_, **kw), True)
        pool = self.dma_sems[q]
        o.sem = pool[self.dma_rr[q] % len(pool)]
        self.dma_rr[q] += 1
        return self._record(o, r, w, False)

    def emit(self):
        seen = {k: {} for k in self.eng_obj}
        for o in self.ops:
            eobj = self.eng_obj[o.eng]
            sn = seen[o.eng]
            need = {}
            for d in o.deps:
                s = d.sem
                if need.get(s, 0) < d.sigval:
                    need[s] = d.sigval
            if o.dma:
                s = o.sem
                if s.count > need.get(s, 0):
                    need[s] = s.count
            for s, v in need.items():
                if sn.get(s, 0) >= v:
                    continue
                eobj.wait_ge(s.h, v)
                sn[s] = v
            ins = o.fn()
            if o.dma:
                s = o.sem
                s.count += 16
                ins.then_inc(s.h, 16)
                o.sigval = s.count
            elif o.sig:
                s = self.eng_sem[o.eng]
                s.count += 1
                ins.then_inc(s.h, 1)
                o.sem = s
                o.sigval = s.count
            o.fn = None
        sp = self.nc.sync
        for q in self.dma_sems:
            for s in self.dma_sems[q]:
                if s.count:
                    sp.wait_ge(s.h, s.count)


def build_program(n_prompt_tiles=N_PROMPT_TILES, do_sample=True, dumps=None):
    nc = bass.Bass("TRN2", target_bir_lowering=False)
    P = Prog(nc)

    def din(name, shape, dt=F32):
        return nc.dram_tensor(name, list(shape), dt, kind="ExternalInput").ap()

    def dout(name, shape, dt=F32):
        return nc.dram_tensor(name, list(shape), dt, kind="ExternalOutput").ap()

    def dscr(name, shape, dt):
        return nc.dram_tensor(name, list(shape), dt, kind="Internal").ap()

    def dump(name, ap, bufs):
        if dumps is None or name in dumps:
            return
        dumps.add(name)
        d = nc.dram_tensor("dbg_" + name, list(ap.shape), ap.dtype, kind="ExternalOutput").ap()
        P.dma("sp", d, ap, r=bufs)

    xp = din("xp", [SEQ, D])
    xs = din("xs", [128, D])
    ckT = din("ckT", [DEPTH, 2, 128, H, PAST])
    cv = din("cv", [DEPTH, 2, PAST, H, 128])
    scT = din("scT", [DEPTH, 2, 128, 8, 30])
    wg = din("wg", [DEPTH, NGRAN, 128, 8 * 512])
    pp_d = din("pp", [128, DEPTH * PP_L])
    rv_d = din("rv", [RV_L0 + DEPTH * RV_LSZ])
    cosx = din("cosx", [SEQ + DSEQ, 256])
    sinx = din("sinx", [SEQ + DSEQ, 256])
    ident_d = din("ident", [128, 128])

    y_p = dout("y_p", [SEQ, D])
    y_s = dout("y_s", [128, D])
    nk_p = dout("nk_p", [DEPTH, SEQ, D])
    nv_p = dout("nv_p", [DEPTH, SEQ, D])
    ncv_p = dout("ncv_p", [DEPTH, 30, D])
    nk_s = dout("nk_s", [DEPTH, 128, D])
    nv_s = dout("nv_s", [DEPTH, 128, D])
    ncv_s = dout("ncv_s", [DEPTH, 2, 30, D])

    wbf = dscr("wbf", [DEPTH, NGRAN, 128, 8 * 512], BF16)
    kTs = dscr("kTs", [DEPTH, H, 128, SEQ], BF16)
    vsc = dscr("vsc", [DEPTH, SEQ, H, 128], BF16)
    B_wbf = [[Buf(f"wbf{l}_{g}") for g in range(NGRAN)] for l in range(DEPTH)]
    B_kTs = [[Buf(f"kTs{l}_{t}") for t in range(N_PROMPT_TILES)] for l in range(DEPTH)]
    B_vsc = [[Buf(f"vsc{l}_{t}") for t in range(N_PROMPT_TILES * 4)] for l in range(DEPTH)]

    def sb(name, shape, dt):
        return nc.alloc_sbuf_tensor(name, list(shape), dt)

    XR = sb("XR", [128, 4, D], F32)
    B_XR = [[Buf(f"XR{s}a"), Buf(f"XR{s}b")] for s in range(4)]
    XT = sb("XT", [128, 8, 512], BF16)
    B_XT = [Buf(f"XT{s}") for s in range(4)]
    CONVMEM = sb("CONVMEM", [128, 8 * 542 + 8 * 512], F32)
    GLU = CONVMEM[:, 0:8 * 542].rearrange("p (a b) -> p a b", a=8)
    B_GLU = [Buf(f"GLU{j}") for j in range(8)]
    ACC = CONVMEM[:, 8 * 542:8 * 542 + 8 * 512].rearrange("p (a b) -> p a b", a=8)
    B_ACC = [Buf(f"ACC{j}") for j in range(8)]
    HT = CONVMEM[:, 0:8192].bitcast(BF16).rearrange("p (a b) -> p a b", a=32)
    TMPF = sb("TMPF", [128, 3, 512], F32)
    B_TMPF = [Buf(f"TMPF{i}") for i in range(3)]
    QKVs = [sb(f"QKV{i}", [128, 2048], F32) for i in range(2)]
    B_QKVs = [Buf(f"QKV{i}") for i in range(2)]
    QKV, B_QKV = QKVs[0], B_QKVs[0]
    QTZ = sb("QTZ", [128, 2, 8, 512], BF16)
    B_QT = [Buf(f"QT{s}") for s in range(4)]
    KTC = sb("KTC", [128, 8, 512], BF16)
    B_KTC = [Buf(f"KTC{s}") for s in range(4)]
    VC = sb("VC", [128, 4, 8, 130], BF16)
    B_VC = [Buf(f"VC{s}") for s in range(4)]
    NBLK = KCHUNK // 128
    KVMEM = sb("KVMEM", [128, 2 * KCHUNK + 2 * NBLK * 130], BF16)
    KCH = [KVMEM[:, i * KCHUNK:(i + 1) * KCHUNK] for i in range(2)]
    VCH = [KVMEM[:, 2 * KCHUNK + i * NBLK * 130: 2 * KCHUNK + (i + 1) * NBLK * 130].rearrange("p (a b) -> p a b", a=NBLK)
           for i in range(2)]
    MT = KVMEM[:, 0:4096].rearrange("p (a b) -> p a b", a=8)
    B_KCH = [Buf(f"KCH{i}") for i in range(2)]
    B_VCH = [Buf(f"VCH{i}") for i in range(2)]
    PTMEM = sb("PTMEM", [128, 4 * 512], BF16)
    PT = [PTMEM[:, i * 512:(i + 1) * 512] for i in range(4)]
    B_PT = [Buf(f"PT{i}") for i in range(4)]
    RT = PTMEM[:, :].bitcast(F32).rearrange("p (a b) -> p a b", a=4)
    OFMEM = sb("OFMEM", [128, 512], F32)
    OF = OFMEM[:, :].rearrange("p (a b) -> p a b", a=4)
    B_OF = Buf("OF")
    CS = OFMEM[:, :].rearrange("p (a b) -> p a b", a=2)
    TMPO = sb("TMPO", [128, 2, 128], F32)
    B_TMPO = [Buf("TMPO0"), Buf("TMPO1")]
    SM = sb("SM", [128, 64], F32)
    B_SM = Buf("SM")
    OBMEM = sb("OBMEM", [128, 4 * D], BF16)
    OB = OBMEM[:, :].rearrange("p (a b) -> p a b", a=4)
    HCT = OBMEM[:, :].rearrange("p (a b) -> p a b", a=8)
    B_OB = [Buf(f"OB{s}") for s in range(4)]
    XBs = [sb(f"XB{i}", [128, D], BF16) for i in range(2)]
    B_XBs = [Buf(f"XB{i}") for i in range(2)]
    GB = sb("GB", [128, 5, D], F32)
    B_GB = [Buf(f"GB{i}") for i in range(5)]
    WB = [sb(f"WB{i}", [128, 8, 512], BF16) for i in range(NWBUF)]
    B_WB = [Buf(f"WB{i}") for i in range(NWBUF)]
    B_CS = Buf("CS")
    B_RT = Buf("RT")
    PPt = sb("PPt", [128, DEPTH * PP_L], F32)
    B_PP = Buf("PP")
    HIST = sb("HIST", [128, DEPTH, 8, 30], F32)
    B_HIST = [Buf(f"HIST{l}") for l in range(DEPTH)]
    IDF = sb("IDF", [128, 128], F32)
    IDB = sb("IDB", [128, 128], BF16)
    B_ID = Buf("ID")
    ONES = sb("ONES", [128, 128], F32)
    MH = sb("MH", [128, 4], F32)
    B_CONST = Buf("CONST")
    ROWS = QKV[0:1, :].rearrange("p (a b) -> p a b", a=4)
    B_ROWS = B_QKV
    STAT = sb("STAT", [128, 4, 2, 6], F32)
    MV = sb("MV", [128, 4, 4], F32)
    B_STAT = Buf("STAT")
    B_STATs = [Buf(f"STAT{i}") for i in range(4)]
    LAMT = sb("LAMT", [128, DEPTH, 4], F32)
    B_LAM = Buf("LAM")
    LTMP = sb("LTMP", [128, 4, 64], F32)
    B_LTMP = Buf("LTMP")
    GN = sb("GN", [128, DEPTH, 128], F32)
    B_GN = Buf("GN")
    B_HT = [Buf(f"HT{f}") for f in range(32)]
    conv_bufs = tuple(B_GLU + B_ACC)
    for b in B_HT:
        b.alias = conv_bufs
    for b in conv_bufs:
        b.alias = tuple(B_HT)
    B_MT = [Buf(f"MT{s}") for s in range(4)]
    kv_bufs = tuple(B_KCH + B_VCH)
    for b in B_MT:
        b.alias = kv_bufs
    for b in kv_bufs:
        b.alias = tuple(B_MT)
    B_RT.alias = tuple(B_PT)
    for b in B_PT:
        b.alias = (B_RT,)
    B_CS.alias = (B_OF,)
    B_OF.alias = (B_CS,)
    EPSC = sb("EPSC", [128, 1], F32)
    print("SBUF bytes remaining:", nc.sbuf_bytes_remaining)
    OT = KTC
    B_OT = B_KTC
    B_HCT = [Buf(f"HCT{j}") for j in range(8)]
    for b in B_HCT:
        b.alias = tuple(B_OB)
    for b in B_OB:
        b.alias = tuple(B_HCT)

    PB = [nc.alloc_psum_tensor(f"pb{i}", [128, 512], F32) for i in range(8)]
    B_PB = [Buf(f"pb{i}") for i in range(8)]
    rr = {"bank": 0, "tmpf": 0, "xb": 0}

    def next_xb():
        i = rr["xb"] % 2
        rr["xb"] += 1
        return XBs[i], B_XBs[i]

    def next_bank():
        i = rr["bank"] % 8
        rr["bank"] += 1
        return i

    def next_tmpf():
        i = rr["tmpf"] % 3
        rr["tmpf"] += 1
        return i

    T_ = nc.tensor
    V_ = nc.vector
    A_ = nc.scalar
    G_ = nc.gpsimd

    P.dma("sp", PPt[:], pp_d[:, :], w=[B_PP])
    P.dma("sp", IDF[:], ident_d[:, :], w=[B_ID])
    P.op("dve", partial(V_.tensor_copy, out=IDB[:], in_=IDF[:]), r=[B_ID], w=[B_ID])
    P.op("dve", partial(V_.memset, ONES[:], 1.0), w=[B_CONST])
    P.op("dve", partial(V_.memset, MH[:], -0.5), w=[B_CONST])
    P.op("dve", partial(V_.memset, EPSC[:], EPS), w=[B_CONST])
    P.op("pool", partial(G_.memset, QTZ[:].rearrange("p a b c -> p (a b c)"), 0.0), w=B_QT)
    P.op("dve", partial(V_.memset, VC[:, :, :, 128:130], 1.0), w=B_VC)
    for l in range(DEPTH):
        P.op("dve", partial(V_.memset, HIST[:, l], 0.0), w=[B_HIST[l]])
    for l in range(DEPTH):
        lam_init = 0.8 - 0.6 * math.exp(-0.3 * l)
        base = RV_L0 + l * RV_LSZ
        for i, off in enumerate((RV_LQ1, RV_LK1, RV_LQ2, RV_LK2)):
            P.dma("sp", LTMP[:, i, :], rv_d[base + off: base + off + 64].partition_broadcast(128), w=[B_LTMP])
        P.dma("sp", GN[:, l, :], rv_d[base + RV_ANG: base + RV_ANG + 128].partition_broadcast(128), w=[B_GN])
        P.op("dve", partial(V_.tensor_tensor, out=TMPO[:, 1, 0:64], in0=LTMP[:, 0, :], in1=LTMP[:, 1, :], op=ALU.mult),
             r=[B_LTMP], w=[B_TMPO[1]])
        P.op("dve", partial(V_.reduce_sum, out=LAMT[:, l, 2:3], in_=TMPO[:, 1, 0:64], axis=mybir.AxisListType.X),
             r=[B_TMPO[1]], w=[B_LAM])
        P.op("dve", partial(V_.tensor_tensor, out=TMPO[:, 1, 0:64], in0=LTMP[:, 2, :], in1=LTMP[:, 3, :], op=ALU.mult),
             r=[B_LTMP], w=[B_TMPO[1]])
        P.op("dve", partial(V_.reduce_sum, out=LAMT[:, l, 3:4], in_=TMPO[:, 1, 0:64], axis=mybir.AxisListType.X),
             r=[B_TMPO[1]], w=[B_LAM])
        P.op("act", partial(A_.activation, out=LAMT[:, l, 2:4], in_=LAMT[:, l, 2:4], func=AF.Exp), r=[B_LAM], w=[B_LAM])
        P.op("dve", partial(V_.tensor_tensor, out=LAMT[:, l, 0:1], in0=LAMT[:, l, 2:3], in1=LAMT[:, l, 3:4], op=ALU.subtract),
             r=[B_LAM], w=[B_LAM])
        P.op("dve", partial(V_.tensor_scalar, out=LAMT[:, l, 0:1], in0=LAMT[:, l, 0:1], scalar1=lam_init, scalar2=None, op0=ALU.add),
             r=[B_LAM], w=[B_LAM])
        P.op("dve", partial(V_.tensor_scalar, out=LAMT[:, l, 1:2], in0=LAMT[:, l, 0:1], scalar1=-1.0, scalar2=None, op0=ALU.mult),
             r=[B_LAM], w=[B_LAM])
        P.op("dve", partial(V_.tensor_scalar, out=GN[:, l, :], in0=GN[:, l, :], scalar1=1.0 - lam_init, scalar2=None, op0=ALU.mult),
             r=[B_GN], w=[B_GN])

    ws = {"conv_next": [0, 0], "load_next": 0, "seq": [], "first_pass": True}
    tile_seq = [(l, g) for l in range(DEPTH) for g in range(NGRAN)]
    n_tiles_total = (1 if do_sample else 0) + n_prompt_tiles
    full_seq = tile_seq * n_tiles_total
    wstate = {"loaded": 0, "conv": 0, "cons": 0, "rel": 0}

    def w_convert(upto):
        while wstate["conv"] < min(upto, len(tile_seq)):
            l, g = tile_seq[wstate["conv"]]
            P.dma("pool", wbf[l, g].rearrange("p (a b) -> p a b", a=8), wg[l, g].rearrange("p (a b) -> p a b", a=8), w=[B_wbf[l][g]])
            wstate["conv"] += 1

    def w_prefetch():
        lim = min(wstate["rel"] + NWBUF, len(full_seq))
        w_convert(lim + 3)
        while wstate["loaded"] < lim:
            j = wstate["loaded"]
            ll, gg = full_seq[j]
            P.dma("sp", WB[j % NWBUF][:].rearrange("p a b -> p (a b)"), wbf[ll, gg],
                  r=[B_wbf[ll][gg]], w=[B_WB[j % NWBUF]])
            wstate["loaded"] += 1

    def w_get(l, g):
        i = wstate["cons"]
        assert full_seq[i] == (l, g), (full_seq[i], l, g)
        w_prefetch()
        assert wstate["loaded"] > i, "weight granule held too long"
        wstate["cons"] += 1
        return WB[i % NWBUF], B_WB[i % NWBUF]

    def w_done(n=1):
        wstate["rel"] += n
        assert wstate["rel"] <= wstate["cons"]
        w_prefetch()

    w_convert(NWBUF + 3)

    def pp(l, off, n=1):
        return PPt[:, l * PP_L + off: l * PP_L + off + n]

    def layer_norm_one(s, gi, bi, xr=None, xb=None):
        if xr is None:
            xr = XR[:, s, :]
            xb = B_XR[s]
        for i in range(2):
            P.op("dve", partial(V_.bn_stats, out=STAT[:, s, i, :], in_=xr[:, i * 512:(i + 1) * 512]),
                 r=[*xb], w=[B_STATs[s]])
        P.op("dve", partial(V_.bn_aggr, out=MV[:, s, 0:2], in_=STAT[:, s].rearrange("p a b -> p (a b)")),
             r=[B_STATs[s]], w=[B_STATs[s]])
        P.op("act", partial(A_.activation, out=MV[:, s, 2:3], in_=MV[:, s, 1:2], func=AF.Sqrt, bias=EPSC[:, :]),
             r=[B_STATs[s], B_CONST], w=[B_STATs[s]])
        P.op("dve", partial(V_.reciprocal, out=MV[:, s, 2:3], in_=MV[:, s, 2:3]), r=[B_STATs[s]], w=[B_STATs[s]])
        P.op("dve", partial(V_.scalar_tensor_tensor, out=MV[:, s, 3:4], in0=MV[:, s, 0:1], scalar=-1.0, in1=MV[:, s, 2:3],
                            op0=ALU.mult, op1=ALU.mult), r=[B_STATs[s]], w=[B_STATs[s]])
        P.op("act", partial(A_.activation, out=xr, in_=xr, func=AF.Identity, scale=MV[:, s, 2:3], bias=MV[:, s, 3:4]),
             r=[*xb, B_STATs[s]], w=[*xb])
        P.op("pool", partial(G_.tensor_tensor, out=xr, in0=xr, in1=GB[:, gi, :], op=ALU.mult),
             r=[*xb, B_GB[gi]], w=[*xb])
        P.op("dve", partial(V_.tensor_tensor, out=xr[:, 0:576], in0=xr[:, 0:576], in1=GB[:, bi, 0:576], op=ALU.add),
             r=[xb[0], B_GB[bi]], w=[xb[0]])
        P.op("pool", partial(G_.tensor_tensor, out=xr[:, 576:1024], in0=xr[:, 576:1024], in1=GB[:, bi, 576:1024], op=ALU.add),
             r=[xb[1], B_GB[bi]], w=[xb[1]])

    def make_xT_cast(s):
        XB, B_XB = next_xb()
        P.op("act", partial(A_.copy, out=XB[:], in_=XR[:, s, :]), r=[*B_XR[s]], w=[B_XB])
        return s, XB, B_XB

    def make_xT_tr(s, XB, B_XB):
        bk = next_bank()
        pv = PB[bk][:].bitcast(BF16).rearrange("p (a b) -> p a b", a=8)
        for j in range(8):
            P.op("pe", partial(T_.transpose, out=pv[:, j, :], in_=XB[:, j * 128:(j + 1) * 128], identity=IDB[:]),
                 r=[B_XB, B_ID], w=[B_PB[bk]])
        P.op("act", partial(A_.copy, out=XT[:, :, s * 128:(s + 1) * 128], in_=pv), r=[B_PB[bk]], w=[B_XT[s]])

    def make_xT_all(slist):
        pend = []
        for idx, s in enumerate(slist):
            pend.append(make_xT_cast(s))
            if idx >= 1:
                make_xT_tr(*pend[idx - 1])
        make_xT_tr(*pend[-1])

    def load_gb(l):
        base = RV_L0 + l * RV_LSZ
        for i, off in enumerate((RV_LN1G, RV_LN1B, RV_LN2G, RV_LN2B, RV_BD)):
            P.dma("sp", GB[:, i, :], rv_d[base + off: base + off + D].partition_broadcast(128), w=[B_GB[i]])

    def qkv_view(s):
        return QKVs[s // 2][:, (s % 2) * 1024:(s % 2 + 1) * 1024], [B_QKVs[s // 2], B_QKVs[s // 2]]

    def run_tile(kind, t, prefetched=False, nxt=None):
        nsub = 4 if kind == "p" else 1
        T = 128 * nsub
        nseg = 1 if kind == "p" else 2
        segn = T // nseg
        segw = 30 + segn
        row0 = t * 512 if kind == "p" else 0
        x_src = xp if kind == "p" else xs
        SS = list(range(nsub))

        def gluv(jc, off, n):
            if nseg == 1:
                return GLU[:, jc, off:off + n]
            return GLU[:, jc, 0:nseg * segw].rearrange("p (s w) -> p s w", s=nseg)[:, :, off:off + n]

        def segview(ap2d):
            if nseg == 1:
                return ap2d
            return ap2d.rearrange("p (s w) -> p s w", s=nseg)

        if prefetched:
            for s in SS:
                qv_, qb_ = qkv_view(s)
                P.op("act", partial(A_.copy, out=XR[:, s, :], in_=qv_), r=qb_[:1], w=[*B_XR[s]])
        else:
            P.dma("sp", GB[:, 2, :], rv_d[RV_LNIN_G:RV_LNIN_G + D].partition_broadcast(128), w=[B_GB[2]])
            P.dma("sp", GB[:, 3, :], rv_d[RV_LNIN_B:RV_LNIN_B + D].partition_broadcast(128), w=[B_GB[3]])
            for s in SS:
                P.dma("sp", XR[:, s, :], x_src[row0 + s * 128: row0 + (s + 1) * 128, :], w=[*B_XR[s]])
            for s in SS:
                layer_norm_one(s, 2, 3)

        for l in range(DEPTH):
            lam_init = 0.8 - 0.6 * math.exp(-0.3 * l)
            last_layer = (l == DEPTH - 1)
            nk_dst = nk_p if kind == "p" else nk_s
            nv_dst = nv_p if kind == "p" else nv_s
            load_gb(l)
            units = []
            loads = []
            for hh in range(8):
                hunits = []
                if kind == "p":
                    npast = row0
                    for k0 in range(0, npast, KCHUNK):
                        n = min(KCHUNK, npast - k0)
                        li = len(loads)
                        par = li % 2
                        tlist = list(range(k0 // 512, (k0 + n + 511) // 512))

                        def ld(par=par, k0=k0, n=n, hh=hh, tlist=tlist):
                            P.dma("sp", KCH[par][:, 0:n], kTs[l, hh, :, k0:k0 + n],
                                  r=[B_kTs[l][tt] for tt in tlist], w=[B_KCH[par]])
                            P.dma("sp", VCH[par][:, 0:n // 128, 0:128],
                                  vsc[l, k0:k0 + n, hh, :].rearrange("(b p) e -> p b e", p=128),
                                  r=[B_vsc[l][bb] for bb in range(k0 // 128, (k0 + n) // 128)], w=[B_VCH[par]])
                        loads.append(ld)
                        for j in range(n // 128):
                            for c in range(2):
                                hunits.append(dict(h=hh, c=c, load=li, kp0=0, nk=128,
                                                   lhsT=KCH[par][:, j * 128:(j + 1) * 128],
                                                   rk=[B_KCH[par]], vrhs=VCH[par][:, j, :], rv=[B_VCH[par]],
                                                   q0=0, q1=512, pv=[(s, s * 128, 128, 0) for s in range(4)], mask=None))
                    for kb in range(4):
                        for c in range(2):
                            hunits.append(dict(h=hh, c=c, load=None, kp0=0, nk=128,
                                               lhsT=KTC[:, hh, kb * 128:(kb + 1) * 128],
                                               rk=[B_KTC[kb]], vrhs=VC[:, kb, hh, :], rv=[B_VC[kb]],
                                               q0=kb * 128, q1=512, pv=[(s, s * 128, 128, 0) for s in range(kb, 4)],
                                               mask=(64, 128, kb * 128, kb * 128 + 64)))
                else:
                    for b in range(2):
                        for k0 in range(0, PAST, KCHUNK):
                            n = KCHUNK
                            li = len(loads)
                            par = li % 2

                            def ld(par=par, k0=k0, n=n, hh=hh, b=b):
                                P.dma("pool", KCH[par][:, 0:n], ckT[l, b, :, hh, k0:k0 + n], w=[B_KCH[par]])
                                P.dma("pool", VCH[par][:, 0:n // 128, 0:128],
                                      cv[l, b, k0:k0 + n, hh, :].rearrange("(b p) e -> p b e", p=128), w=[B_VCH[par]])
                            loads.append(ld)
                            for j in range(n // 128):
                                for c in range(2):
                                    hunits.append(dict(h=hh, c=c, load=li, kp0=0, nk=128,
                                                       lhsT=KCH[par][:, j * 128:(j + 1) * 128],
                                                       rk=[B_KCH[par]], vrhs=VCH[par][:, j, :], rv=[B_VCH[par]],
                                                       q0=b * 64, q1=b * 64 + 64, pv=[(0, 0, 128, 0)], mask=None,
                                                       pt=2 * b + (j % 2), ptc0=b * 64))
                        for c in range(2):
                            hunits.append(dict(h=hh, c=c, load=None, kp0=b * 64, nk=64,
                                               lhsT=KTC[:, hh, b * 64:(b + 1) * 64],
                                               rk=[B_KTC[0]], vrhs=VC[b * 64:(b + 1) * 64, 0, hh, :], rv=[B_VC[0]],
                                               q0=b * 64, q1=b * 64 + 64, pv=[(0, 0, 128, 0)], mask=None,
                                               pt=2 * b, ptc0=b * 64))
                hunits[-1]["last_of_head"] = True
                seen_start = set()
                for u in hunits:
                    fl = []
                    for (s, qa, qn, op0) in u["pv"]:
                        key = (s, op0)
                        fl.append(key not in seen_start)
                        seen_start.add(key)
                    u["start"] = fl
                units.extend(hunits)
            nun = len(units)
            conv_per_unit = -(-(8 * CW + 1) // nun)
            loads_done = [0]

            def ensure_load(li):
                while loads_done[0] <= li and loads_done[0] < len(loads):
                    loads[loads_done[0]]()
                    loads_done[0] += 1

            for i in range(2):
                P.op("dve", partial(V_.memset, VCH[i][:, :, 128:130], 1.0), w=[B_VCH[i]])
            if loads:
                ensure_load(0)
            if len(loads) > 1:
                ensure_load(1)
            make_xT_all(SS)
            if kind == "p":
                P.op("pool", partial(G_.tensor_copy, out=GLU[:, :, 0:30], in_=HIST[:, l]), r=[B_HIST[l]], w=B_GLU)
            else:
                for b in range(2):
                    P.dma("sp", GLU[:, :, b * segw: b * segw + 30], scT[l, b], w=B_GLU)
            for half in range(2):
                wa, bwa = w_get(l, G_A0 + 2 * half)
                wb_, bwb = w_get(l, G_B0 + 2 * half)
                for j in range(4):
                    jc = half * 4 + j
                    ba = next_bank()
                    bb = next_bank()
                    for kc in range(8):
                        P.op("pe", partial(T_.matmul, PB[ba][:, 0:T], lhsT=wa[:, kc, j * 128:(j + 1) * 128], rhs=XT[:, kc, 0:T],
                                           start=(kc == 0), stop=(kc == 7)), r=[bwa] + B_XT[:nsub], w=[B_PB[ba]])
                    for kc in range(8):
                        P.op("pe", partial(T_.matmul, PB[bb][:, 0:T], lhsT=wb_[:, kc, j * 128:(j + 1) * 128], rhs=XT[:, kc, 0:T],
                                           start=(kc == 0), stop=(kc == 7)), r=[bwb] + B_XT[:nsub], w=[B_PB[bb]])
                    ti = next_tmpf()
                    P.op("act", partial(A_.activation, out=TMPF[:, ti, 0:T], in_=PB[bb][:, 0:T], func=AF.Sigmoid),
                         r=[B_PB[bb]], w=[B_TMPF[ti]])
                    P.op("dve", partial(V_.tensor_tensor, out=gluv(jc, 30, segn), in0=segview(PB[ba][:, 0:T]),
                                        in1=segview(TMPF[:, ti, 0:T]), op=ALU.mult),
                         r=[B_PB[ba], B_TMPF[ti]], w=[B_GLU[jc]])
                w_done(2)
            def conv_ops():
                for jc in range(8):
                    accv = segview(ACC[:, jc, 0:T])
                    cwb = PP_CW + jc * 31
                    yield partial(P.op, "dve", partial(V_.tensor_scalar, out=accv, in0=gluv(jc, 0, segn),
                                                       scalar1=pp(l, cwb), scalar2=pp(l, PP_CB + jc),
                                                       op0=ALU.mult, op1=ALU.add),
                                  r=[B_GLU[jc], B_PP], w=[B_ACC[jc]])
                    for k in range(1, CW):
                        yield partial(P.op, "dve", partial(V_.scalar_tensor_tensor, out=accv, in0=gluv(jc, k, segn),
                                                           scalar=pp(l, cwb + k), in1=accv, op0=ALU.mult, op1=ALU.add),
                                      r=[B_GLU[jc], B_ACC[jc], B_PP], w=[B_ACC[jc]], nosync=(segn >= 512))
                if kind == "p":
                    yield partial(P.op, "pool", partial(G_.tensor_copy, out=HIST[:, l], in_=GLU[:, :, 512:542]),
                                  r=B_GLU, w=[B_HIST[l]])
            conv_it = conv_ops()
            n_conv_ops = 8 * CW + 1

            conv_state = {"done": 0}

            def emit_conv(k):
                for _ in range(k):
                    f = next(conv_it, None)
                    if f is None:
                        return
                    f()
                    conv_state["done"] += 1

            gq = [w_get(l, G_Q0 + i) for i in range(4)]

            def qk_mm(s):
                QKV, B_QKV = QKVs[s % 2], B_QKVs[s % 2]
                for i in range(4):
                    bk = next_bank()
                    for kc in range(8):
                        P.op("pe", partial(T_.matmul, PB[bk][:, :], lhsT=XT[:, kc, s * 128:(s + 1) * 128], rhs=gq[i][0][:, kc, :],
                                           start=(kc == 0), stop=(kc == 7)), r=[gq[i][1], B_XT[s]], w=[B_PB[bk]])
                    P.op("act", partial(A_.copy, out=QKV[:, i * 512:(i + 1) * 512], in_=PB[bk][:, :]), r=[B_PB[bk]], w=[B_QKV])

            def qk_post(s):
                QKV, B_QKV = QKVs[s % 2], B_QKVs[s % 2]
                if kind == "p":
                    P.dma("sp", CS[:, 0, :], cosx[row0 + s * 128: row0 + (s + 1) * 128, :], w=[B_CS])
                    P.dma("sp", CS[:, 1, :], sinx[row0 + s * 128: row0 + (s + 1) * 128, :], w=[B_CS])
                else:
                    for b in range(2):
                        P.dma("sp", CS[b * 64:(b + 1) * 64, 0, :], cosx[SEQ:SEQ + 64, :], w=[B_CS])
                        P.dma("sp", CS[b * 64:(b + 1) * 64, 1, :], sinx[SEQ:SEQ + 64, :], w=[B_CS])
                qv = QKV[:].rearrange("p (g d) -> p g d", d=64)
                t1 = qv[:, :, 0:8]
                t2 = qv[:, :, 8:16]
                cosv = CS[:, 0, :].rearrange("p (g i) -> p g i", i=8)
                sinv = CS[:, 1, :].rearrange("p (g i) -> p g i", i=8)
                rt = [RT[:, i, :].rearrange("p (g i) -> p g i", i=8) for i in range(4)]
                rr_ = [B_QKV, B_CS]
                P.op("pool", partial(G_.tensor_tensor, out=rt[0], in0=t1, in1=cosv, op=ALU.mult), r=rr_, w=[B_RT])
                P.op("pool", partial(G_.tensor_tensor, out=rt[1], in0=t2, in1=sinv, op=ALU.mult), r=rr_, w=[B_RT])
                P.op("pool", partial(G_.tensor_tensor, out=rt[2], in0=t1, in1=sinv, op=ALU.mult), r=rr_, w=[B_RT])
                P.op("pool", partial(G_.tensor_tensor, out=rt[3], in0=t2, in1=cosv, op=ALU.mult), r=rr_, w=[B_RT])
                P.op("pool", partial(G_.tensor_tensor, out=t1, in0=rt[0], in1=rt[1], op=ALU.subtract), r=[B_RT], w=[B_QKV])
                P.op("pool", partial(G_.tensor_tensor, out=t2, in0=rt[3], in1=rt[2], op=ALU.add), r=[B_RT], w=[B_QKV])
                P.dma("pool", nk_dst[l, row0 + s * 128: row0 + (s + 1) * 128, :], QKV[:, 1024:2048], r=[B_QKV])
                xbs = []
                for part in range(2):
                    XB, B_XB = next_xb()
                    P.op("dve", partial(V_.tensor_copy, out=XB[:], in_=QKV[:, part * 1024:(part + 1) * 1024]), r=[B_QKV], w=[B_XB])
                    xbs.append((XB, B_XB))
                for part, (dst, bdst) in enumerate(((None, B_QT), (KTC, B_KTC))):
                    XB, B_XB = xbs[part]
                    bk = next_bank()
                    pv = PB[bk][:].bitcast(BF16).rearrange("p (a b) -> p a b", a=8)
                    for hh in range(8):
                        c0 = hh * 128
                        P.op("pe", partial(T_.transpose, out=pv[:, hh, :], in_=XB[:, c0:c0 + 128], identity=IDB[:]),
                             r=[B_XB, B_ID], w=[B_PB[bk]])
                    if part == 0:
                        P.op("act", partial(A_.copy, out=QTZ[0:64, 0, :, s * 128:(s + 1) * 128], in_=pv[0:64]), r=[B_PB[bk]], w=[bdst[s]])
                        P.op("dve", partial(V_.tensor_copy, out=QTZ[64:128, 1, :, s * 128:(s + 1) * 128], in_=pv[64:128]),
                             r=[B_PB[bk]], w=[bdst[s]])
                    else:
                        P.op("act", partial(A_.copy, out=dst[:, :, s * 128:(s + 1) * 128], in_=pv), r=[B_PB[bk]], w=[bdst[s]])

            qk_mm(0)
            for s in SS:
                if s + 1 < nsub:
                    qk_mm(s + 1)
                qk_post(s)
                emit_conv(20 if kind == "p" else 40)
            w_done(4)
            gv = [w_get(l, G_V0 + i) for i in range(2)]
            for s in SS:
                QKV, B_QKV = QKVs[s % 2], B_QKVs[s % 2]
                for i in range(2):
                    bk = next_bank()
                    for kc in range(8):
                        P.op("pe", partial(T_.matmul, PB[bk][:, :], lhsT=XT[:, kc, s * 128:(s + 1) * 128], rhs=gv[i][0][:, kc, :],
                                           start=(kc == 0), stop=(kc == 7)), r=[gv[i][1], B_XT[s]], w=[B_PB[bk]])
                    P.op("act", partial(A_.copy, out=QKV[:, i * 512:(i + 1) * 512], in_=PB[bk][:, :]), r=[B_PB[bk]], w=[B_QKV])
                P.dma("pool", nv_dst[l, row0 + s * 128: row0 + (s + 1) * 128, :], QKV[:, 0:1024], r=[B_QKV])
                P.op("dve", partial(V_.tensor_copy, out=VC[:, s, :, 0:128], in_=QKV[:, 0:1024].rearrange("p (h e) -> p h e", h=8)),
                     r=[B_QKV], w=[B_VC[s]])
                emit_conv(12 if kind == "p" else 24)
            w_done(2)
            if kind == "p" and t < N_PROMPT_TILES - 1:
                P.dma("pool", kTs[l, :, :, row0:row0 + 512].rearrange("h p k -> p h k"), KTC[:, :, :],
                      r=B_KTC, w=[B_kTs[l][t]])
                for s in SS:
                    P.dma("pool", vsc[l, row0 + s * 128:row0 + (s + 1) * 128].rearrange("p h e -> p h e"), VC[:, s, :, 0:128],
                          r=[B_VC[s]], w=[B_vsc[l][t * 4 + s]])

            def emit_qk(i):
                u = units[i]
                if u["load"] is not None:
                    ensure_load(u["load"])
                bk = i % 4
                c = u["c"]
                n = u["q1"] - u["q0"]
                P.op("pe", partial(T_.matmul, PB[bk][u["kp0"]:u["kp0"] + u["nk"], 0:n], lhsT=u["lhsT"],
                                   rhs=QTZ[:, c, u["h"], u["q0"]:u["q1"]], start=True, stop=True),
                     r=u["rk"] + B_QT[:nsub], w=[B_PB[bk]])

            def emit_exp(i):
                u = units[i]
                bk = i % 4
                n = u["q1"] - u["q0"]
                kp0, nk = u["kp0"], u["nk"]
                pk = u.get("pt", bk)
                pc0 = u.get("ptc0", 0)
                P.op("act", partial(A_.activation, out=PT[pk][kp0:kp0 + nk, pc0:pc0 + n], in_=PB[bk][kp0:kp0 + nk, 0:n],
                                    func=AF.Exp, scale=0.125), r=[B_PB[bk]], w=[B_PT[pk]])
                if u["mask"] is not None:
                    p0, p1, c0, c1 = u["mask"]
                    P.op("pool", partial(G_.memset, PT[bk][p0:p1, c0 - u["q0"]:c1 - u["q0"]], 0.0), w=[B_PT[bk]])

            def emit_pv(i):
                u = units[i]
                bk = i % 4
                kp0, nk = u["kp0"], u["nk"]
                c = u["c"]
                pk = u.get("pt", bk)
                qoff = u["q0"] if "pt" not in u else 0
                for (s, qa, qn, op0), st in zip(u["pv"], u["start"]):
                    accb = PB[4 + s][:, 0:260].rearrange("p (c e) -> p c e", c=2)
                    P.op("pe", partial(T_.matmul, accb[op0:op0 + qn, c, :], lhsT=PT[pk][kp0:kp0 + nk, qa - qoff:qa - qoff + qn],
                                       rhs=u["vrhs"], start=st, stop=True, skip_group_check=True),
                         r=[B_PT[pk]] + u["rv"], w=[B_PB[4 + s]])

            def finalize_head(hh):
                for s in SS:
                    accb = PB[4 + s][:, 0:260].rearrange("p (c e) -> p c e", c=2)
                    o = 8 * s
                    P.op("dve", partial(V_.reciprocal, out=SM[:, o:o + 2], in_=accb[:, :, 128]), r=[B_PB[4 + s]], w=[B_SM])
                    P.op("dve", partial(V_.tensor_scalar, out=SM[:, o + 2:o + 3], in0=SM[:, o + 1:o + 2], scalar1=LAMT[:, l, 1:2],
                                        scalar2=None, op0=ALU.mult), r=[B_SM, B_LAM], w=[B_SM])
                    P.op("dve", partial(V_.tensor_scalar, out=TMPO[:, s % 2, :], in0=accb[:, 1, 0:128], scalar1=SM[:, o + 2:o + 3],
                                        scalar2=None, op0=ALU.mult), r=[B_PB[4 + s], B_SM], w=[B_TMPO[s % 2]])
                    P.op("dve", partial(V_.scalar_tensor_tensor, out=OF[:, s, :], in0=accb[:, 0, 0:128], scalar=SM[:, o:o + 1],
                                        in1=TMPO[:, s % 2, :], op0=ALU.mult, op1=ALU.add),
                         r=[B_PB[4 + s], B_SM, B_TMPO[s % 2]], w=[B_OF])
                for s in SS:
                    P.op("dve", partial(V_.tensor_tensor, out=TMPO[:, s % 2, :], in0=OF[:, s, :], in1=OF[:, s, :], op=ALU.mult),
                         r=[B_OF], w=[B_TMPO[s % 2]])
                    P.op("dve", partial(V_.reduce_sum, out=SM[:, 32 + s:33 + s], in_=TMPO[:, s % 2, :], axis=mybir.AxisListType.X),
                         r=[B_TMPO[s % 2]], w=[B_SM])
                P.op("pool", partial(G_.tensor_scalar, out=SM[:, 40:40 + nsub], in0=SM[:, 32:32 + nsub], scalar1=1.0 / 128, scalar2=EPS,
                                     op0=ALU.mult, op1=ALU.add), r=[B_SM], w=[B_SM])
                P.op("pool", partial(G_.tensor_tensor, out=SM[:, 40:40 + nsub], in0=SM[:, 40:40 + nsub], in1=MH[:, 0:nsub], op=ALU.pow),
                     r=[B_SM, B_CONST], w=[B_SM])
                for s in SS:
                    P.op("dve", partial(V_.scalar_tensor_tensor, out=OB[:, s, hh * 128:(hh + 1) * 128], in0=OF[:, s, :],
                                        scalar=SM[:, 40 + s:41 + s], in1=GN[:, l, :], op0=ALU.mult, op1=ALU.mult),
                         r=[B_OF, B_SM, B_GN], w=[B_OB[s]])

            if kind == "s":
                for b in range(2):
                    for j in range(2):
                        P.op("pool", partial(G_.memset, PT[2 * b + j][:, (1 - b) * 64:(1 - b) * 64 + 64], 0.0), w=[B_PT[2 * b + j]])
            emit_qk(0)
            if nun > 1:
                emit_qk(1)
            for i in range(nun):
                u = units[i]
                if u["load"] is not None and u["load"] + 1 < len(loads):
                    ensure_load(u["load"] + 1)
                emit_exp(i)
                if i + 2 < nun:
                    emit_qk(i + 2)
                emit_pv(i)
                if i == 0:
                    conv_left = n_conv_ops - conv_state["done"]
                    conv_base = conv_state["done"]
                tgt = conv_base + -(-(i + 1) * conv_left // nun)
                emit_conv(max(0, tgt - conv_state["done"]))
                if u.get("last_of_head"):
                    finalize_head(u["h"])
            emit_conv(n_conv_ops)

            dump(f"acc_{kind}{t}_{l}", ACC[:, :, 0:T], B_ACC)
            dump(f"ob_{kind}{t}_{l}", OB[:, 0:nsub, :], B_OB[:nsub])
            dump(f"lam_{kind}{t}_{l}", LAMT[:, l, :], [B_LAM])
            def emit_new_conv(src_of_jc, dst_ap):
                bk0 = next_bank()
                bk1 = next_bank()
                for jc in range(8):
                    bk = bk0 if jc < 4 else bk1
                    P.op("pe", partial(T_.transpose, out=PB[bk][0:30, (jc % 4) * 128:(jc % 4 + 1) * 128], in_=src_of_jc(jc),
                                       identity=IDF[:]), r=B_GLU + [B_HIST[l], B_ID], w=[B_PB[bk]])
                for hf, bkx in enumerate((bk0, bk1)):
                    ti = next_tmpf()
                    P.op("act", partial(A_.copy, out=TMPF[0:30, ti, :], in_=PB[bkx][0:30, :]), r=[B_PB[bkx]], w=[B_TMPF[ti]])
                    P.dma("pool", dst_ap[:, hf * 512:(hf + 1) * 512], TMPF[0:30, ti, :], r=[B_TMPF[ti]])
            if kind == "p" and t == n_prompt_tiles - 1:
                emit_new_conv(lambda jc: HIST[:, l, jc, :], ncv_p[l])
            if kind == "s":
                for b in range(2):
                    emit_new_conv(lambda jc, b=b: GLU[:, jc, b * segw + segn: b * segw + segw], ncv_s[l, b])

            bs1 = next_bank()
            bs2 = next_bank()
            for jc in range(8):
                ti = next_tmpf()
                P.op("pool", partial(G_.tensor_tensor, out=TMPF[:, ti, 0:T], in0=ACC[:, jc, 0:T], in1=ACC[:, jc, 0:T], op=ALU.mult),
                     r=[B_ACC[jc]], w=[B_TMPF[ti]])
                P.op("pe", partial(T_.matmul, PB[bs1][0:1, 0:T], lhsT=ONES[:, 0:1], rhs=ACC[:, jc, 0:T], start=(jc == 0), stop=(jc == 7)),
                     r=[B_ACC[jc], B_CONST], w=[B_PB[bs1]])
                P.op("pe", partial(T_.matmul, PB[bs2][0:1, 0:T], lhsT=ONES[:, 0:1], rhs=TMPF[:, ti, 0:T], start=(jc == 0), stop=(jc == 7)),
                     r=[B_TMPF[ti], B_CONST], w=[B_PB[bs2]])
            R = lambda i: ROWS[0:1, i, 0:T]
            P.op("dve", partial(V_.tensor_scalar, out=R(0), in0=PB[bs1][0:1, 0:T], scalar1=1.0 / D, scalar2=None, op0=ALU.mult),
                 r=[B_PB[bs1]], w=[B_ROWS])
            P.op("dve", partial(V_.tensor_tensor, out=R(1), in0=R(0), in1=R(0), op=ALU.mult), r=[B_ROWS], w=[B_ROWS])
            P.op("dve", partial(V_.scalar_tensor_tensor, out=R(1), in0=PB[bs2][0:1, 0:T], scalar=1.0 / D, in1=R(1),
                                op0=ALU.mult, op1=ALU.subtract), r=[B_PB[bs2], B_ROWS], w=[B_ROWS])
            P.op("act", partial(A_.activation, out=R(2), in_=R(1), func=AF.Sqrt, bias=EPSC[0:1, :]), r=[B_ROWS, B_CONST], w=[B_ROWS])
            P.op("dve", partial(V_.reciprocal, out=R(2), in_=R(2)), r=[B_ROWS], w=[B_ROWS])
            P.op("dve", partial(V_.scalar_tensor_tensor, out=R(0), in0=R(0), scalar=-1.0, in1=R(2), op0=ALU.mult, op1=ALU.mult),
                 r=[B_ROWS], w=[B_ROWS])
            for s in SS:
                bk = next_bank()
                pv = PB[bk][:].bitcast(BF16).rearrange("p (a b) -> p a b", a=8)
                for hh in range(8):
                    P.op("pe", partial(T_.transpose, out=pv[:, hh, :], in_=OB[:, s, hh * 128:(hh + 1) * 128], identity=IDB[:]),
                         r=[B_OB[s], B_ID], w=[B_PB[bk]])
                P.op("act", partial(A_.copy, out=OT[:, :, s * 128:(s + 1) * 128], in_=pv), r=[B_PB[bk]], w=[B_OT[s]])

            P.op("pe", partial(T_.matmul, PB[bs1][:, 0:T], lhsT=ONES[0:1, :], rhs=R(2), start=True, stop=True),
                 r=[B_ROWS, B_CONST], w=[B_PB[bs1]])
            P.op("pe", partial(T_.matmul, PB[bs2][:, 0:T], lhsT=ONES[0:1, :], rhs=R(0), start=True, stop=True),
                 r=[B_ROWS, B_CONST], w=[B_PB[bs2]])
            for jc in range(8):
                P.op("dve", partial(V_.tensor_tensor, out=ACC[:, jc, 0:T], in0=ACC[:, jc, 0:T], in1=PB[bs1][:, 0:T], op=ALU.mult),
                     r=[B_ACC[jc], B_PB[bs1]], w=[B_ACC[jc]])
                P.op("dve", partial(V_.tensor_tensor, out=ACC[:, jc, 0:T], in0=ACC[:, jc, 0:T], in1=PB[bs2][:, 0:T], op=ALU.add),
                     r=[B_ACC[jc], B_PB[bs2]], w=[B_ACC[jc]])
                P.op("act", partial(A_.activation, out=HCT[:, jc, 0:T], in_=ACC[:, jc, 0:T], func=AF.Silu,
                                    scale=pp(l, PP_LG + jc), bias=pp(l, PP_LB + jc)), r=[B_ACC[jc], B_PP], w=[B_HCT[jc]])

            dump(f"hct_{kind}{t}_{l}", HCT[:, :, 0:T], B_HCT)
            dump(f"ot_{kind}{t}_{l}", OT[:, :, 0:T], B_OT[:nsub])
            if last_layer and nxt is not None:
                nkind, nt = nxt
                nsrc = xp if nkind == "p" else xs
                nrow0 = nt * 512 if nkind == "p" else 0
                for s2 in range(4 if nkind == "p" else 1):
                    qv_, qb_ = qkv_view(s2)
                    P.dma("sp", qv_, nsrc[nrow0 + s2 * 128: nrow0 + (s2 + 1) * 128, :], w=qb_[:1])
            for half in range(2):
                ggc = w_get(l, G_GC0 + 4 * half)
                gga = w_get(l, G_GA0 + 4 * half)
                gco = w_get(l, G_CO0 + 4 * half)
                gao = w_get(l, G_AO0 + 4 * half)
                for j in range(4):
                    jc = half * 4 + j
                    banks = [next_bank() for _ in range(4)]
                    srcs = ((ggc, XT, B_XT[:nsub]), (gga, XT, B_XT[:nsub]), (gco, HCT, B_HCT), (gao, OT, B_OT[:nsub]))
                    for (gw, src, bsrc), bk in zip(srcs, banks):
                        for kc in range(8):
                            P.op("pe", partial(T_.matmul, PB[bk][:, 0:T], lhsT=gw[0][:, kc, j * 128:(j + 1) * 128], rhs=src[:, kc, 0:T],
                                               start=(kc == 0), stop=(kc == 7)), r=[gw[1]] + list(bsrc), w=[B_PB[bk]])
                    t0 = next_tmpf()
                    t1_ = next_tmpf()
                    P.op("act", partial(A_.activation, out=TMPF[:, t0, 0:T], in_=PB[banks[0]][:, 0:T], func=AF.Sigmoid,
                                        bias=pp(l, PP_BG + jc)), r=[B_PB[banks[0]], B_PP], w=[B_TMPF[t0]])
                    P.op("act", partial(A_.activation, out=TMPF[:, t1_, 0:T], in_=PB[banks[1]][:, 0:T], func=AF.Sigmoid,
                                        bias=pp(l, PP_BG + 8 + jc)), r=[B_PB[banks[1]], B_PP], w=[B_TMPF[t1_]])
                    P.op("dve", partial(V_.tensor_tensor, out=TMPF[:, t0, 0:T], in0=PB[banks[2]][:, 0:T], in1=TMPF[:, t0, 0:T], op=ALU.mult),
                         r=[B_PB[banks[2]], B_TMPF[t0]], w=[B_TMPF[t0]])
                    P.op("dve", partial(V_.tensor_tensor, out=TMPF[:, t1_, 0:T], in0=PB[banks[3]][:, 0:T], in1=TMPF[:, t1_, 0:T], op=ALU.mult),
                         r=[B_PB[banks[3]], B_TMPF[t1_]], w=[B_TMPF[t1_]])
                    P.op("pool", partial(G_.tensor_tensor, out=MT[:, jc, 0:T], in0=TMPF[:, t0, 0:T], in1=TMPF[:, t1_, 0:T], op=ALU.add),
                         r=[B_TMPF[t0], B_TMPF[t1_]], w=B_MT[:nsub])
                w_done(4)
            dump(f"mt_{kind}{t}_{l}", MT[:, :, 0:T], B_MT[:nsub])
            gwo = [w_get(l, G_WO0 + i) for i in range(2)]
            for s in SS:
                for i in range(2):
                    bk = next_bank()
                    for kc in range(8):
                        P.op("pe", partial(T_.matmul, PB[bk][:, :], lhsT=MT[:, kc, s * 128:(s + 1) * 128], rhs=gwo[i][0][:, kc, :],
                                           start=(kc == 0), stop=(kc == 7)), r=[gwo[i][1]] + B_MT[:nsub], w=[B_PB[bk]])
                    P.op("dve", partial(V_.scalar_tensor_tensor, out=XR[:, s, i * 512:(i + 1) * 512], in0=XR[:, s, i * 512:(i + 1) * 512],
                                        scalar=ALPHA, in1=PB[bk][:, :], op0=ALU.mult, op1=ALU.add),
                         r=[*B_XR[s], B_PB[bk]], w=[*B_XR[s]])
                layer_norm_one(s, 0, 1)
            w_done(2)
            make_xT_all(SS)
            if last_layer and nxt is not None:
                P.dma("sp", GB[:, 0, :], rv_d[RV_LNIN_G:RV_LNIN_G + D].partition_broadcast(128), w=[B_GB[0]])
                P.dma("sp", GB[:, 1, :], rv_d[RV_LNIN_B:RV_LNIN_B + D].partition_broadcast(128), w=[B_GB[1]])
                early_ln = [s2 for s2 in range(4 if nxt[0] == "p" else 1)]
            else:
                early_ln = []
            dump(f"x1_{kind}{t}_{l}", XR[:, 0:nsub, :], [b_ for p_ in B_XR[:nsub] for b_ in p_])
            for gi in range(8):
                gu = w_get(l, G_UP + gi)
                for j in range(4):
                    f = gi * 4 + j
                    bk = next_bank()
                    for kc in range(8):
                        P.op("pe", partial(T_.matmul, PB[bk][:, 0:T], lhsT=gu[0][:, kc, j * 128:(j + 1) * 128], rhs=XT[:, kc, 0:T],
                                           start=(kc == 0), stop=(kc == 7)), r=[gu[1]] + B_XT[:nsub], w=[B_PB[bk]])
                    ti = next_tmpf()
                    P.op("act", partial(A_.activation, out=TMPF[:, ti, 0:T], in_=PB[bk][:, 0:T], func=AF.Relu, bias=pp(l, PP_BU + f)),
                         r=[B_PB[bk], B_PP], w=[B_TMPF[ti]])
                    P.op("pool", partial(G_.tensor_tensor, out=HT[:, f, 0:T], in0=TMPF[:, ti, 0:T], in1=TMPF[:, ti, 0:T], op=ALU.mult),
                         r=[B_TMPF[ti]], w=[B_HT[f]])
                w_done(1)
                if gi % 2 == 1 and early_ln:
                    s2 = early_ln.pop(0)
                    qv_, qb_ = qkv_view(s2)
                    layer_norm_one(s2, 0, 1, xr=qv_, xb=qb_)
            for i in range(2):
                banks = [4 * i + s for s in SS]
                gds = [w_get(l, G_DN + i * 4 + kq) for kq in range(4)]
                for s in SS:
                    for kq in range(4):
                        gd = gds[kq]
                        for kc in range(8):
                            f = kq * 8 + kc
                            P.op("pe", partial(T_.matmul, PB[banks[s]][:, :], lhsT=HT[:, f, s * 128:(s + 1) * 128], rhs=gd[0][:, kc, :],
                                               start=(f == 0), stop=(f == 31)), r=[gd[1], B_HT[f]], w=[B_PB[banks[s]]])
                    xs_ = XR[:, s, i * 512:(i + 1) * 512]
                    P.op("dve", partial(V_.scalar_tensor_tensor, out=xs_, in0=xs_, scalar=ALPHA, in1=PB[banks[s]][:, :],
                                        op0=ALU.mult, op1=ALU.add), r=[*B_XR[s], B_PB[banks[s]]], w=[*B_XR[s]])
                    if i == 1:
                        P.op("pool", partial(G_.tensor_tensor, out=XR[:, s, :], in0=XR[:, s, :], in1=GB[:, 4, :], op=ALU.add),
                             r=[*B_XR[s], B_GB[4]], w=[*B_XR[s]])
                        layer_norm_one(s, 2, 3)
                        if s == nsub - 1:
                            dump(f"x2_{kind}{t}_{l}", XR[:, 0:nsub, :], [b_ for p_ in B_XR[:nsub] for b_ in p_])
                        if last_layer:
                            y_dst = y_p if kind == "p" else y_s
                            P.dma("pool", y_dst[row0 + s * 128: row0 + (s + 1) * 128, :], XR[:, s, :], r=[*B_XR[s]])
                w_done(4)
            rr["bank"] = 0

    order = [("p", t) for t in range(n_prompt_tiles)] + ([("s", 0)] if do_sample else [])
    for i, (k_, t_) in enumerate(order):
        run_tile(k_, t_, prefetched=(i > 0), nxt=(order[i + 1] if i + 1 < len(order) else None))
    P.emit()
    return nc


def _offset_of(handle):
    for attr in ("offset", "addr", "address", "start"):
        if hasattr(handle, attr):
            v = getattr(handle, attr)
            v = v() if callable(v) else v
            if isinstance(v, int):
                return v
    raise RuntimeError("cannot find sbuf offset: " + str([a for a in dir(handle) if not a.startswith("_")]))


def _granules(w_in, w_conv_out, w_attn_out, w_out, w_up, w_down):
    out = np.empty((DEPTH, NGRAN, 128, 8, 512), np.float32)

    def g1024(W):
        n = W.shape[1] // 512
        return W.reshape(8, 128, n, 512).transpose(2, 1, 0, 3)
    for l in range(DEPTH):
        gi = g1024(w_in[l])
        out[l, G_A0], out[l, G_A1], out[l, G_B0], out[l, G_B1] = gi[0], gi[1], gi[2], gi[3]
        out[l, G_Q0:G_Q0 + 6] = gi[4:10]
        out[l, G_GC0], out[l, G_GC1], out[l, G_GA0], out[l, G_GA1] = gi[10], gi[11], gi[12], gi[13]
        co = g1024(w_conv_out[l])
        ao = g1024(w_attn_out[l])
        out[l, G_CO0], out[l, G_CO1] = co[0], co[1]
        out[l, G_AO0], out[l, G_AO1] = ao[0], ao[1]
        out[l, G_WO0:G_WO0 + 2] = g1024(w_out[l])
        out[l, G_UP:G_UP + 8] = g1024(w_up[l])
        dn = w_down[l].reshape(4, 8, 128, 2, 512).transpose(3, 0, 2, 1, 4)
        out[l, G_DN:G_DN + 8] = dn.reshape(8, 128, 8, 512)
    return out.reshape(DEPTH, NGRAN, 128, 4096)


def _host_prep(inp):
    f = lambda k: np.asarray(inp[k], dtype=np.float32)
    wgr = _granules(f("w_in"), f("w_conv_out"), f("w_attn_out"), f("w_out"), f("w_up"), f("w_down"))
    pp = np.empty((128, DEPTH * PP_L), np.float32)
    for l in range(DEPTH):
        o = l * PP_L
        pp[:, o + PP_BG:o + PP_BG + 16] = f("b_gate")[l].reshape(16, 128).T
        pp[:, o + PP_CW:o + PP_CW + 248] = f("conv_dw_w")[l].reshape(CW, 8, 128).transpose(2, 1, 0).reshape(128, 248)
        pp[:, o + PP_CB:o + PP_CB + 8] = f("conv_dw_b")[l].reshape(8, 128).T
        pp[:, o + PP_LG:o + PP_LG + 8] = f("conv_ln_g")[l].reshape(8, 128).T
        pp[:, o + PP_LB:o + PP_LB + 8] = f("conv_ln_b")[l].reshape(8, 128).T
        pp[:, o + PP_BU:o + PP_BU + 32] = f("b_up")[l].reshape(32, 128).T
    rv = np.empty((RV_L0 + DEPTH * RV_LSZ,), np.float32)
    rv[RV_LNIN_G:RV_LNIN_G + D] = f("ln_in_g")
    rv[RV_LNIN_B:RV_LNIN_B + D] = f("ln_in_b")
    for l in range(DEPTH):
        b = RV_L0 + l * RV_LSZ
        for off, k, n in ((RV_LN1G, "ln1_g", D), (RV_LN1B, "ln1_b", D), (RV_LN2G, "ln2_g", D), (RV_LN2B, "ln2_b", D),
                          (RV_BD, "b_down", D), (RV_ANG, "attn_norm_g", 128), (RV_LQ1, "lambda_q1", 64),
                          (RV_LK1, "lambda_k1", 64), (RV_LQ2, "lambda_q2", 64), (RV_LK2, "lambda_k2", 64)):
            rv[b + off:b + off + n] = f(k)[l]
    pos = np.concatenate([np.arange(SEQ), PAST + np.arange(DSEQ)]).astype(np.float32)
    inv = (np.float32(THETA) ** (-np.arange(8, dtype=np.float32) * np.float32(2.0) / np.float32(16))).astype(np.float32)
    ang = (pos[:, None] * inv[None, :]).astype(np.float32)
    cosx = np.tile(np.cos(ang).astype(np.float32), (1, 32))
    sinx = np.tile(np.sin(ang).astype(np.float32), (1, 32))
    ident = np.eye(128, dtype=np.float32)
    shared = dict(wg=wgr, pp=pp, rv=rv, cosx=cosx, sinx=sinx, ident=ident)
    xp_all = f("x_prompt")
    xs_all = f("x_sample")
    ck = f("cache_k")
    cvv = f("cache_v")
    sc = f("state_conv")
    in_maps = []
    for c in range(NCORES):
        m = dict(shared)
        m["xp"] = np.ascontiguousarray(xp_all[c])
        m["xs"] = np.ascontiguousarray(xs_all[2 * c:2 * c + 2].reshape(128, D))
        m["ckT"] = np.ascontiguousarray(ck[:, 2 * c:2 * c + 2].transpose(0, 1, 4, 3, 2))
        m["cv"] = np.ascontiguousarray(cvv[:, 2 * c:2 * c + 2])
        m["scT"] = np.ascontiguousarray(sc[:, 2 * c:2 * c + 2].reshape(DEPTH, 2, 30, 8, 128).transpose(0, 1, 4, 3, 2))
        in_maps.append(m)
    return in_maps


def _assemble(results):
    y_p = np.stack([r["y_p"] for r in results])
    y_s = np.concatenate([r["y_s"].reshape(2, DSEQ, D) for r in results], 0)
    nk_p = np.stack([r["nk_p"] for r in results], 1).reshape(DEPTH, NCORES, SEQ, H, 128)
    nv_p = np.stack([r["nv_p"] for r in results], 1).reshape(DEPTH, NCORES, SEQ, H, 128)
    ncv_p = np.stack([r["ncv_p"] for r in results], 1)
    nk_s = np.concatenate([r["nk_s"].reshape(DEPTH, 2, DSEQ, H, 128) for r in results], 1)
    nv_s = np.concatenate([r["nv_s"].reshape(DEPTH, 2, DSEQ, H, 128) for r in results], 1)
    ncv_s = np.concatenate([r["ncv_s"] for r in results], 1)
    return tuple(np.ascontiguousarray(a, dtype=np.float32) for a in (y_p, y_s, nk_p, nv_p, ncv_p, nk_s, nv_s, ncv_s))


def kernel(**inputs):
    in_maps = _host_prep(inputs)
    nc = build_program()
    res = run_bass_kernel_spmd(nc, in_maps, core_ids=list(range(NCORES)))
    return _assemble(res.results)
```
